# Optimizing a Trainium2 kernel written in Bass

```python
import math
import jax, jax.numpy as jnp
from jax import lax
import numpy as np

D_MODEL = 2048
BATCH = 2
SEQ = 4096
DEPTH = 2
DEC_BATCH = 32
DEC_SEQ = 4
PAST_LEN = 8192
PAGE_SIZE = 128

N_BRANCH = 4
BRANCH_W = D_MODEL // N_BRANCH
CHUNK = 128
A_GROUPS = 4
A_GDIM = BRANCH_W // A_GROUPS
CONV_W = 31
H_C = 4
HD = BRANCH_W // H_C
Q_BLOCK = 128
FORGET_BIAS = 3.0
N_MEM = 256
H_M = 4
HD_M = BRANCH_W // H_M
IN_COLS = 12 * BRANCH_W + H_C + N_BRANCH * D_MODEL
ALPHA = (2 * DEPTH) ** 0.25
BETA = (8 * DEPTH) ** -0.25
LN_EPS = 1e-5

kernel_name = 'hybrid_gated_fox_gmlp_conv_step'


def layer_norm(x, g, b):
    xf = x.astype(jnp.float32)
    mu = jnp.mean(xf, axis=-1, keepdims=True)
    var = jnp.mean(jnp.square(xf - mu), axis=-1, keepdims=True)
    return ((xf - mu) * lax.rsqrt(var + LN_EPS) * g.astype(jnp.float32) + b.astype(jnp.float32)).astype(x.dtype)


def split_columns(z):
    widths = (BRANCH_W,) * 9 + (H_C,) + (BRANCH_W,) * 3
    idx, s = [], 0
    for w in widths:
        s += w
        idx.append(s)
    return jnp.split(z, idx, axis=-1)


def chunk_sgu(u_pre, v_pre, ln_g, ln_b, w_s, b_s):
    bn, t, _ = u_pre.shape
    c = min(t, CHUNK)
    nc = t // c
    u = jax.nn.gelu(u_pre)
    v = layer_norm(jax.nn.gelu(v_pre), ln_g, ln_b)
    mask = jnp.tril(jnp.ones((c, c), dtype=bool))
    ws = jnp.where(mask[None], w_s[:, :c, :c], 0.0)
    vc = v.reshape(bn, nc, c, A_GROUPS, A_GDIM)
    s = jnp.einsum('gts,bnsgd->bntgd', ws, vc) + b_s[:, :c].T[None, None, :, :, None]
    return u * s.reshape(bn, t, BRANCH_W), v


def conformer_conv(a, b, buf, w_dw, b_dw, ln_g, ln_b, w_pw, b_pw):
    h = a * jax.nn.sigmoid(b)
    hp = jnp.concatenate([buf.astype(h.dtype), h], axis=1)
    y = lax.conv_general_dilated(hp, w_dw[:, None, :].astype(h.dtype), window_strides=(1,), padding='VALID',
                                 dimension_numbers=('NWC', 'WIO', 'NWC'), feature_group_count=BRANCH_W) + b_dw
    y = jax.nn.silu(layer_norm(y, ln_g, ln_b))
    return y @ w_pw + b_pw, hp[:, -(CONV_W - 1):]


def fox_attention(q, k_all, v_all, c_all, n_past):
    bn, t = q.shape[:2]
    s_len = k_all.shape[1]
    qb = min(t, Q_BLOCK)
    nb = t // qb
    q_blocks = q.reshape(bn, nb, qb, H_C, HD).transpose(1, 0, 2, 3, 4)
    cq_blocks = c_all[:, n_past:].reshape(bn, nb, qb, H_C).transpose(1, 0, 3, 2)
    q_pos = (n_past + jnp.arange(t)).reshape(nb, qb)
    k_pos = jnp.arange(s_len)
    ck = jnp.transpose(c_all, (0, 2, 1))

    def one_block(args):
        q_blk, cq, pos = args
        logits = jnp.einsum('bqhd,bkhd->bhqk', q_blk, k_all, preferred_element_type=jnp.float32) * (HD ** -0.5)
        logits = logits + (cq[..., None] - ck[:, :, None, :])
        logits = jnp.where(k_pos[None, None, None, :] <= pos[None, None, :, None], logits, -jnp.inf)
        p = jax.nn.softmax(logits, axis=-1).astype(v_all.dtype)
        return jnp.einsum('bhqk,bkhd->bqhd', p, v_all)

    out = lax.map(one_block, (q_blocks, cq_blocks, q_pos))
    return out.transpose(1, 0, 2, 3, 4).reshape(bn, t, H_C * HD)


def memory_attention(q, mem_k, mem_v):
    bn, t = q.shape[:2]
    logits = jnp.einsum('bqhd,bmhd->bhqm', q, mem_k, preferred_element_type=jnp.float32) * (HD_M ** -0.5)
    p = jax.nn.softmax(logits, axis=-1).astype(mem_v.dtype)
    return jnp.einsum('bhqm,bmhd->bqhd', p, mem_v).reshape(bn, t, H_M * HD_M)


def hybrid_layer(x, conv_buf, past, mem_k, mem_v, w_in, ln_v_g, ln_v_b, w_s, b_s, w_dw, b_dw,
                 ln_c_g, ln_c_b, w_pw, b_pw, b_f, w_branch, w_out, ln_g, ln_b):
    bn, t, _ = x.shape
    (u_a, v_a, g_a, a_b, b_b, g_b, q_c, k_c, v_c, f_c, g_c, q_m, g_m, gate_pre) = split_columns(x @ w_in)
    out_a, v_rows = chunk_sgu(u_a, v_a, ln_v_g, ln_v_b, w_s, b_s)
    if conv_buf is None:
        conv_buf = jnp.zeros((bn, CONV_W - 1, BRANCH_W), x.dtype)
    out_b, new_buf = conformer_conv(a_b, b_b, conv_buf, w_dw, b_dw, ln_c_g, ln_c_b, w_pw, b_pw)
    q = q_c.reshape(bn, t, H_C, HD)
    k = k_c.reshape(bn, t, H_C, HD)
    v = v_c.reshape(bn, t, H_C, HD)
    logf = jax.nn.log_sigmoid(f_c.astype(jnp.float32) + b_f.astype(jnp.float32))
    if past is None:
        k_all, v_all, logf_all, n_past = k, v, logf, 0
    else:
        pk, pv, plf = past
        k_all = jnp.concatenate([pk.astype(k.dtype), k], axis=1)
        v_all = jnp.concatenate([pv.astype(v.dtype), v], axis=1)
        logf_all = jnp.concatenate([plf.astype(jnp.float32), logf], axis=1)
        n_past = pk.shape[1]
    c_all = jnp.cumsum(logf_all, axis=1)
    out_c = fox_attention(q, k_all, v_all, c_all, n_past)
    out_m = memory_attention(q_m.reshape(bn, t, H_M, HD_M), mem_k, mem_v)
    branches = ((out_a, g_a), (out_b, g_b), (out_c, g_c), (out_m, g_m))
    h = None
    for i, (o, g) in enumerate(branches):
        gate = jax.nn.sigmoid(gate_pre[..., i * D_MODEL:(i + 1) * D_MODEL])
        term = gate * ((o * jax.nn.silu(g)) @ w_branch[i])
        h = term if h is None else h + term
    y = layer_norm(ALPHA * x + h @ w_out, ln_g, ln_b)
    return y, k, v, logf, new_buf, v_rows


def setup_inputs(seed: int = 0) -> dict:
    key = jax.random.key(seed)
    ks = jax.random.split(key, 32)
    n_pages = PAST_LEN // PAGE_SIZE
    n_used = DEC_BATCH * n_pages
    n_pool = n_used + max(1, n_used // 4)

    def nrm(k, shape, scale=1.0):
        return scale * jax.random.normal(k, shape, jnp.float32)

    return {
        'x_prompt': nrm(ks[0], (BATCH, SEQ, D_MODEL)),
        'x_sample': nrm(ks[1], (DEC_BATCH, DEC_SEQ, D_MODEL)),
        'mem_prompt': nrm(ks[2], (BATCH, N_MEM, D_MODEL)),
        'cache_k': nrm(ks[3], (DEPTH, n_pool, PAGE_SIZE, H_C, HD)),
        'cache_v': nrm(ks[4], (DEPTH, n_pool, PAGE_SIZE, H_C, HD)),
        'cache_logf': jax.nn.log_sigmoid(FORGET_BIAS + nrm(ks[5], (DEPTH, n_pool, PAGE_SIZE, H_C))),
        'cache_mem_k': nrm(ks[6], (DEPTH, DEC_BATCH, N_MEM, H_M, HD_M)),
        'cache_mem_v': nrm(ks[7], (DEPTH, DEC_BATCH, N_MEM, H_M, HD_M)),
        'state_conv': nrm(ks[8], (DEPTH, DEC_BATCH, CONV_W - 1, BRANCH_W), 0.5),
        'page_table': jax.random.permutation(ks[9], n_pool)[:n_used].reshape(DEC_BATCH, n_pages).astype(jnp.int32),
        'w_in': nrm(ks[10], (DEPTH, D_MODEL, IN_COLS), D_MODEL ** -0.5),
        'w_mem_k': nrm(ks[11], (DEPTH, D_MODEL, BRANCH_W), D_MODEL ** -0.5),
        'w_mem_v': nrm(ks[12], (DEPTH, D_MODEL, BRANCH_W), D_MODEL ** -0.5),
        'ln_v_g': 1.0 + nrm(ks[13], (DEPTH, BRANCH_W), 0.02),
        'ln_v_b': nrm(ks[14], (DEPTH, BRANCH_W), 0.02),
        'w_s': nrm(ks[15], (DEPTH, A_GROUPS, CHUNK, CHUNK), CHUNK ** -0.5),
        'b_s': 1.0 + nrm(ks[16], (DEPTH, A_GROUPS, CHUNK), 0.1),
        'w_dw': nrm(ks[17], (DEPTH, CONV_W, BRANCH_W), CONV_W ** -0.5),
        'b_dw': nrm(ks[18], (DEPTH, BRANCH_W), 0.02),
        'ln_c_g': 1.0 + nrm(ks[19], (DEPTH, BRANCH_W), 0.02),
        'ln_c_b': nrm(ks[20], (DEPTH, BRANCH_W), 0.02),
        'w_pw': nrm(ks[21], (DEPTH, BRANCH_W, BRANCH_W), BRANCH_W ** -0.5),
        'b_pw': nrm(ks[22], (DEPTH, BRANCH_W), 0.02),
        'b_f': FORGET_BIAS + nrm(ks[23], (DEPTH, H_C), 0.1),
        'w_branch': nrm(ks[24], (DEPTH, N_BRANCH, BRANCH_W, D_MODEL), BETA * BRANCH_W ** -0.5),
        'w_out': nrm(ks[25], (DEPTH, D_MODEL, D_MODEL), BETA * D_MODEL ** -0.5),
        'ln_g': 1.0 + nrm(ks[26], (DEPTH, D_MODEL), 0.02),
        'ln_b': nrm(ks[27], (DEPTH, D_MODEL), 0.02),
    }


def reference(x_prompt, x_sample, mem_prompt, cache_k, cache_v, cache_logf, cache_mem_k, cache_mem_v,
              state_conv, page_table, w_in, w_mem_k, w_mem_v, ln_v_g, ln_v_b, w_s, b_s, w_dw, b_dw,
              ln_c_g, ln_c_b, w_pw, b_pw, b_f, w_branch, w_out, ln_g, ln_b):
    bp = x_prompt.shape[0]
    db = x_sample.shape[0]
    n_pages = page_table.shape[1]
    xp, xs = x_prompt, x_sample
    pk_l, pv_l, plf_l, pconv_l, pmk_l, pmv_l = [], [], [], [], [], []
    sk_l, sv_l, slf_l, sconv_l, schunk_l = [], [], [], [], []
    for l in range(DEPTH):
        lp = (w_in[l], ln_v_g[l], ln_v_b[l], w_s[l], b_s[l], w_dw[l], b_dw[l], ln_c_g[l], ln_c_b[l],
              w_pw[l], b_pw[l], b_f[l], w_branch[l], w_out[l], ln_g[l], ln_b[l])
        mk_p = (mem_prompt @ w_mem_k[l]).reshape(bp, N_MEM, H_M, HD_M)
        mv_p = (mem_prompt @ w_mem_v[l]).reshape(bp, N_MEM, H_M, HD_M)
        xp, pk, pv, plf, pconv, _ = hybrid_layer(xp, None, None, mk_p, mv_p, *lp)
        pk_l.append(pk); pv_l.append(pv); plf_l.append(plf); pconv_l.append(pconv)
        pmk_l.append(mk_p); pmv_l.append(mv_p)
        past = (cache_k[l][page_table].reshape(db, n_pages * PAGE_SIZE, H_C, HD),
                cache_v[l][page_table].reshape(db, n_pages * PAGE_SIZE, H_C, HD),
                cache_logf[l][page_table].reshape(db, n_pages * PAGE_SIZE, H_C))
        xs, sk, sv, slf, sconv, schunk = hybrid_layer(xs, state_conv[l], past, cache_mem_k[l], cache_mem_v[l], *lp)
        sk_l.append(sk); sv_l.append(sv); slf_l.append(slf); sconv_l.append(sconv); schunk_l.append(schunk)
    return (xp, xs,
            jnp.stack(pk_l), jnp.stack(pv_l), jnp.stack(plf_l), jnp.stack(pconv_l),
            jnp.stack(pmk_l), jnp.stack(pmv_l),
            jnp.stack(sk_l), jnp.stack(sv_l), jnp.stack(slf_l), jnp.stack(sconv_l), jnp.stack(schunk_l))
```

```python
import numpy as np
from contextlib import ExitStack
import concourse.bass as bass
import concourse.mybir as mybir
from concourse.bass_utils import run_bass_kernel_spmd

F32 = mybir.dt.float32
BF16 = mybir.dt.bfloat16
I32 = mybir.dt.int32
AF = mybir.ActivationFunctionType
ALU = mybir.AluOpType

ENG = ['pe', 'act', 'dve', 'pool', 'sp']
NDMA = {'sp': 16, 'pool': 12, 'act': 4}

D = 2048
BW = 512
NCOL = 14340
SEQ = 4096
TP = 1024
NPASS = 4
NS = 16
DEPTH = 2
NPOOL = 2560
NPG = 64
ALPHA = (2 * DEPTH) ** 0.25
EPS = 1e-5
SCALE = 128 ** -0.5
OFF = dict(u_a=0, v_a=512, g_a=1024, a_b=1536, b_b=2048, g_b=2560, q_c=3072, k_c=3584, v_c=4096,
           f_c=4608, g_c=4612, q_m=5124, g_m=5636, gate=6148)
C_ID = 0
C_TRI = 128
C_ONE = 256
C_BLKM = 384
C_UT = 400
C_IOTA = 464
CW = 465


class Res:
    __slots__ = ('name', 'w', 'r', 'excl')

    def __init__(self, name, excl=False):
        self.name = name
        self.w = None
        self.r = {}
        self.excl = excl


class Prog:
    def __init__(self, nc):
        self.nc = nc
        self.es = ExitStack()
        self.streams = {e: [] for e in ENG}
        self.cnt = {e: 0 for e in ENG}
        self.csem = {e: nc.alloc_semaphore('c_' + e) for e in ENG}
        self.known = {e: {} for e in ENG}
        self.dsem = {q: [nc.alloc_semaphore('d_%s_%d' % (q, i)) for i in range(n)] for q, n in NDMA.items()}
        self.dval = {q: [0] * n for q, n in NDMA.items()}
        self.dnext = {q: 0 for q in NDMA}
        self.nres = 0
        self.nt = 0

    def sb(self, shape, dt, stack=None):
        self.nt += 1
        return (stack or self.es).enter_context(self.nc.sbuf_tensor('t%d' % self.nt, list(shape), dt))

    def ps(self, shape, dt=F32):
        self.nt += 1
        return self.es.enter_context(self.nc.psum_tensor('p%d' % self.nt, list(shape), dt))

    def res(self, name=None, excl=False):
        self.nres += 1
        return Res(name or ('r%d' % self.nres), excl)

    def _sem_of(self, key):
        if key[0] == 'e':
            return self.csem[key[1]]
        return self.dsem[key[1]][key[2]]

    def _collect(self, eng, reads, writes, extra=()):
        need = {}
        toks = []
        for r in reads:
            if r.w is not None:
                toks.append(r.w)
        for w in writes:
            if w.w is not None:
                toks.append(w.w)
            toks.extend(w.r.items())
        toks.extend(extra)
        kn = self.known[eng]
        for key, val in toks:
            if key == ('e', 'pe') and eng == 'pe':
                continue
            if kn.get(key, 0) >= val:
                continue
            if need.get(key, 0) < val:
                need[key] = val
        for key, val in need.items():
            kn[key] = val
        return [(self._sem_of(k), v) for k, v in need.items()]

    def _mark(self, tok, reads, writes):
        key, val = tok
        for r in reads:
            r.r[key] = val
        for w in writes:
            w.w = tok
            w.r = {}

    def op(self, eng, fn, reads=(), writes=()):
        if any(r.excl for r in reads):
            writes = list(writes) + [r for r in reads if r.excl]
            reads = [r for r in reads if not r.excl]
        waits = self._collect(eng, reads, writes)
        self.cnt[eng] += 1
        tok = (('e', eng), self.cnt[eng])
        self._mark(tok, reads, writes)
        sem = self.csem[eng]

        def emit(e):
            for s, v in waits:
                e.wait_ge(s, v)
            fn(e).then_inc(sem, 1)
        self.streams[eng].append(emit)

    def dma(self, q, fn, reads=(), writes=()):
        n = len(self.dsem[q])
        slot = self.dnext[q] % n
        self.dnext[q] += 1
        prev = self.dval[q][slot]
        key = ('d', q, slot)
        extra = [(key, prev)] if prev > 0 else []
        waits = self._collect(q, reads, writes, extra)
        newv = prev + 16
        self.dval[q][slot] = newv
        self.known[q][key] = max(self.known[q].get(key, 0), prev)
        self._mark((key, newv), reads, writes)
        sem = self.dsem[q][slot]

        def emit(e):
            for s, v in waits:
                e.wait_ge(s, v)
            with self.nc.allow_non_contiguous_dma(reason="small strided param/state transfers"):
                ins = fn(e)
            ins.then_inc(sem, 16)
        self.streams[q].append(emit)

    def barrier(self):
        waits = []
        for q in NDMA:
            for i, v in enumerate(self.dval[q]):
                if v > 0:
                    waits.append((('d', q, i), v))
        for e in ENG:
            if self.cnt[e] > 0:
                waits.append((('e', e), self.cnt[e]))
        for eng in ENG:
            kn = self.known[eng]
            need = []
            for key, v in waits:
                if key == ('e', eng):
                    continue
                if kn.get(key, 0) >= v:
                    continue
                kn[key] = v
                need.append((self._sem_of(key), v))

            def emit(e, need=need):
                for s, v in need:
                    e.wait_ge(s, v)
            self.streams[eng].append(emit)

    def finish(self):
        self.barrier()
        nc = self.nc
        with nc.Block() as block:
            @block.tensor
            def _(e):
                for c in self.streams['pe']:
                    c(e)

            @block.scalar
            def _(e):
                for c in self.streams['act']:
                    c(e)

            @block.vector
            def _(e):
                for c in self.streams['dve']:
                    c(e)

            @block.gpsimd
            def _(e):
                for c in self.streams['pool']:
                    c(e)

            @block.sync
            def _(e):
                for c in self.streams['sp']:
                    c(e)
        self.es.close()


class _Rec:
    def __getattr__(self, name):
        def mk(*a, **k):
            return lambda e: getattr(e, name)(*a, **k)
        return mk


R = _Rec()


def bc(ap_tensor, offset, dims):
    return bass.AP(ap_tensor, offset, [list(d) for d in dims])


ALL_PH = ('params', 'mem', 'mem_fm', 'mem_tm', 'd1', 'd2', 'd3', 'A', 'B', 'C', 'M', 'merge', 'out', 'ln', 'prompt', 'sample')


def build(DEPTH=DEPTH, NPASS=NPASS, NPG=NPG, NPOOL=NPOOL, PH=ALL_PH):
    SEQ = NPASS * TP
    nc = bass.Bass("TRN2", target_bir_lowering=False)
    P = Prog(nc)

    def din(name, shape, dt=F32):
        return nc.dram_tensor(name, list(shape), dt, kind="ExternalInput").ap()

    def dout(name, shape, dt=F32):
        return nc.dram_tensor(name, list(shape), dt, kind="ExternalOutput").ap()

    def dscr(name, shape, dt=F32):
        return nc.dram_tensor(name, list(shape), dt).ap()

    xp = din("xp", [SEQ, D])
    xs = din("xs", [NS, D])
    memp = din("memp", [256, D])
    cache_k = din("cache_k", [DEPTH, NPOOL, 128, 4, 128]).rearrange("l n r h e -> (l n r) (h e)")
    cache_v = din("cache_v", [DEPTH, NPOOL, 128, 4, 128]).rearrange("l n r h e -> (l n r) (h e)")
    cache_lf = din("cache_logf", [DEPTH, NPOOL, 128, 4]).rearrange("l n r h -> (l n) (r h)")
    cmk = din("cmk", [DEPTH, 4, 256, 512])
    cmv = din("cmv", [DEPTH, 4, 256, 512])
    sconv = din("sconv", [DEPTH, 4, 30, 512])
    pt = din("pt", [4, NPG], I32)
    w_in = din("w_in", [DEPTH, D, NCOL])
    w_mk = din("w_mem_k", [DEPTH, D, BW])
    w_mv = din("w_mem_v", [DEPTH, D, BW])
    ln_v_g = din("ln_v_g", [DEPTH, BW]); ln_v_b = din("ln_v_b", [DEPTH, BW])
    w_s = din("w_s", [DEPTH, 4, 128, 128]); b_s = din("b_s", [DEPTH, 4, 128])
    w_dw = din("w_dw", [DEPTH, 31, BW]); b_dw = din("b_dw", [DEPTH, BW])
    ln_c_g = din("ln_c_g", [DEPTH, BW]); ln_c_b = din("ln_c_b", [DEPTH, BW])
    w_pw = din("w_pw", [DEPTH, BW, BW]); b_pw = din("b_pw", [DEPTH, BW])
    b_f = din("b_f", [DEPTH, 4])
    w_br = din("w_branch", [DEPTH, 4, BW, D])
    w_out = din("w_out", [DEPTH, D, D])
    ln_g = din("ln_g", [DEPTH, D]); ln_b = din("ln_b", [DEPTH, D])
    cst = din("cst", [128, CW])
    cmask = din("cmask", [128, 2048])

    y_p = dout("y_p", [SEQ, D]); y_s = dout("y_s", [NS, D])
    nk_p = dout("nk_p", [DEPTH, SEQ, BW]); nv_p = dout("nv_p", [DEPTH, SEQ, BW])
    nlf_p = dout("nlf_p", [DEPTH, SEQ, 4]); ncv_p = dout("ncv_p", [DEPTH, 30, BW])
    nmk_p = dout("nmk_p", [DEPTH, 256, BW]); nmv_p = dout("nmv_p", [DEPTH, 256, BW])
    nk_s = dout("nk_s", [DEPTH, NS, BW]); nv_s = dout("nv_s", [DEPTH, NS, BW])
    nlf_s = dout("nlf_s", [DEPTH, NS, 4]); ncv_s = dout("ncv_s", [DEPTH, 4, 30, BW])
    nch_s = dout("nch_s", [DEPTH, NS, BW])

    x1_p = dscr("x1_p", [SEQ, D]); x1_s = dscr("x1_s", [NS, D])
    ypre = dscr("ypre", [TP, D])
    kT_h = dscr("kT_h", [BW, SEQ], BF16)
    v_h = dscr("v_h", [SEQ, BW], BF16)
    r_x1p = P.res(); r_x1s = P.res(); r_ypre = P.res(); r_kTh = P.res(); r_vh = P.res()
    r_out = P.res()

    cs = P.sb([128, CW], F32); r_cs = P.res()
    idb = P.sb([128, 128], BF16)
    oneb = P.sb([128, 128], BF16)
    maskb = P.sb([128, 4, 512], BF16)
    blkmb = P.sb([16, 16], BF16)
    r_cb = P.res()
    xT = P.sb([128, 16, TP], BF16); r_xT = P.res()
    hT = P.sb([128, 16, TP], BF16); r_hT = P.res()
    bo = [P.sb([128, 4, TP], BF16) for _ in range(4)]; r_bo = [P.res() for _ in range(4)]
    W = [P.sb([128, 16, 512], BF16) for _ in range(2)]; r_W = [P.res() for _ in range(2)]
    wctr = [0]
    halo = P.sb([128, 4, 30], F32); r_halo = P.res()
    ckh = P.sb([128, 32, 4], F32); r_ckh = P.res()
    carry = P.sb([128, 4], F32); r_carry = P.res()
    kTm = P.sb([128, 4, 256], BF16); vm = P.sb([128, 2, 512], BF16); r_mem = P.res()
    prm = P.sb([128, 4, 8], F32); r_prm = P.res()
    wdw = P.sb([128, 4, 31], F32)
    bsb = P.sb([128, 4, 128], F32)
    wsT = P.sb([128, 4, 128], BF16)
    wsTs = P.sb([16, 4, 16], BF16)
    bss = P.sb([128, 4, 16], F32)
    bfb = P.sb([128, 4], F32)
    wpw = P.sb([128, 4, 512], BF16)
    pm = [P.ps([128, 512]) for _ in range(4)]; r_pm = [P.res(excl=True) for _ in range(4)]
    pa = [P.ps([128, 512]) for _ in range(2)]; r_pa = [P.res(excl=True) for _ in range(2)]
    po = P.ps([128, 512]); r_po = P.res(excl=True)
    pq = P.ps([128, 512]); r_pq = P.res(excl=True)
    pmc = [0]; pac = [0]

    def next_pm():
        i = pmc[0] % 4; pmc[0] += 1
        return pm[i], r_pm[i]

    def next_pa():
        i = pac[0] % 2; pac[0] += 1
        return pa[i], r_pa[i]

    def load_w(src_ap, nk, ncols=512):
        i = wctr[0] % 2; wctr[0] += 1
        P.dma('pool', R.dma_start(out=W[i][:, 0:nk, 0:ncols],
                                            in_=src_ap.rearrange("(k p) c -> p k c", p=128)),
              writes=[r_W[i]])
        return W[i], r_W[i]

    rr = [0]

    def evac_engine():
        rr[0] += 1
        return 'act' if rr[0] % 2 else 'dve'

    P.dma('sp', R.dma_start(out=cs[:], in_=cst[:, :]), writes=[r_cs])
    P.op('dve', R.tensor_copy(out=idb[:], in_=cs[:, C_ID:C_ID + 128]), reads=[r_cs], writes=[r_cb])
    P.op('dve', R.tensor_copy(out=oneb[:], in_=cs[:, C_ONE:C_ONE + 128]), reads=[r_cs], writes=[r_cb])
    with ExitStack() as st0:
        mtmp = P.sb([128, 2048], F32, st0); r_mtmp = P.res()
        P.dma('sp', R.dma_start(out=mtmp[:], in_=cmask[:, :]), writes=[r_mtmp])
        P.op('dve', R.tensor_copy(out=maskb[:].rearrange("p a b -> p (a b)"), in_=mtmp[:]), reads=[r_mtmp], writes=[r_cb])
        P.barrier()
    P.op('dve', R.tensor_copy(out=blkmb[:], in_=cs[0:16, C_BLKM:C_BLKM + 16]), reads=[r_cs], writes=[r_cb])
    ident = cs[:, C_ID:C_ID + 128]
    tri = cs[:, C_TRI:C_TRI + 128]
    onef = cs[:, C_ONE:C_ONE + 128]

    def gelu_to(tmps, out_ap, ps_ap, n, npart, st, reads, writes):
        t1, t2, rt = tmps
        P.op('act', R.activation(out=t1[0:npart, 0:n], in_=ps_ap, func=AF.Square), reads=reads, writes=[rt])
        P.op('dve', R.tensor_scalar(out=t1[0:npart, 0:n], in0=t1[0:npart, 0:n], scalar1=0.044715, scalar2=1.0,
                                              op0=ALU.mult, op1=ALU.add), reads=[rt], writes=[rt])
        P.op('dve', R.tensor_tensor(out=t1[0:npart, 0:n], in0=t1[0:npart, 0:n], in1=ps_ap, op=ALU.mult),
             reads=[rt] + list(reads), writes=[rt])
        P.op('act', R.activation(out=t2[0:npart, 0:n], in_=t1[0:npart, 0:n], func=AF.Sigmoid, scale=1.5957691216057308),
             reads=[rt], writes=[rt])
        P.op('dve', R.tensor_tensor(out=out_ap, in0=t2[0:npart, 0:n], in1=ps_ap, op=ALU.mult),
             reads=[rt] + list(reads), writes=writes)

    for l in range(DEPTH):
        x_in_p = xp if l == 0 else x1_p
        x_in_s = xs if l == 0 else x1_s
        last = (l == DEPTH - 1)
        y_out_p = y_p if last else x1_p
        y_out_s = y_s if last else x1_s
        r_yp = r_out if last else r_x1p
        r_ys = r_out if last else r_x1s
        r_xinp = r_cs if l == 0 else r_x1p
        r_xins = r_cs if l == 0 else r_x1s
        win = w_in[l]

        P.barrier()
        with ExitStack() as st:
            def fm_param(src, j):
                P.dma('sp', R.dma_start(out=prm[:, :, j:j + 1], in_=src.rearrange("(c p o) -> p c o", p=128, o=1)),
                      writes=[r_prm])
            with nc.allow_non_contiguous_dma(reason="small param loads"):
                fm_param(ln_c_g[l], 0); fm_param(ln_c_b[l], 1); fm_param(b_pw[l], 2); fm_param(b_dw[l], 3)
                for cc in range(4):
                    P.dma('sp', R.dma_start(out=wdw[:, cc, :], in_=w_dw[l][:, cc * 128:(cc + 1) * 128].rearrange("j p -> p j")), writes=[r_prm])
                P.dma('sp', R.dma_start(out=bsb[:], in_=b_s[l].partition_broadcast(128)), writes=[r_prm])
                P.dma('sp', R.dma_start(out=bfb[:], in_=b_f[l].partition_broadcast(128)), writes=[r_prm])
                for g in range(4):
                    P.dma('sp', R.dma_start(out=bss[:, g, :].rearrange("p (b t) -> p b t", t=4),
                                                           in_=bc(b_s.tensor, b_s[l, g, 0:4].offset, [[0, 128], [0, 4], [1, 4]])),
                          writes=[r_prm])
            P.dma('pool', R.dma_start(out=wpw[:], in_=w_pw[l].rearrange("(k p) c -> p k c", p=128)), writes=[r_prm])
            wsf = P.sb([128, 4, 128], F32, st); r_wsf = P.res()
            P.dma('sp', R.dma_start(out=wsf[:], in_=w_s[l].rearrange("g t s -> t g s")), writes=[r_wsf])
            ps_, rps = next_pm()
            for g in range(4):
                P.op('pe', R.transpose(out=ps_[:, g * 128:(g + 1) * 128], in_=wsf[:, g, :], identity=ident),
                     reads=[r_wsf, r_cs], writes=[rps])
            for g in range(4):
                P.op('dve', R.tensor_tensor(out=wsT[:, g, :], in0=ps_[:, g * 128:(g + 1) * 128], in1=tri, op=ALU.mult),
                     reads=[rps, r_cs], writes=[r_prm])
            wss = P.sb([16, 4, 16], F32, st); r_wss = P.res()
            P.op('dve', R.memset(wss[:], 0.0), writes=[r_wss])
            with nc.allow_non_contiguous_dma(reason="tiny 4x4 transposed blocks"):
                for b4 in range(4):
                    for g in range(4):
                        P.dma('sp', R.dma_start(
                            out=wss[b4 * 4:b4 * 4 + 4, g, b4 * 4:b4 * 4 + 4],
                            in_=w_s[l, g, 0:4, 0:4].rearrange("t s -> s t")), writes=[r_wss])
            for g in range(4):
                P.op('dve', R.tensor_tensor(out=wsTs[:, g, :], in0=wss[:, g, :], in1=cs[0:16, C_BLKM:C_BLKM + 16], op=ALU.mult),
                     reads=[r_wss, r_cs], writes=[r_prm])
            P.barrier()

        with ExitStack() as st:
          if 'mem' in PH:
            mT = P.sb([128, 16, 256], BF16, st); r_mT = P.res()
            xa = P.sb([128, D], F32, st); r_xa = P.res()
            for tt in range(2):
                P.dma('sp', R.dma_start(out=xa[:], in_=memp[tt * 128:(tt + 1) * 128, :]), writes=[r_xa])
                for k4 in range(4):
                    ps_, rps = next_pm()
                    for j in range(4):
                        kc = k4 * 4 + j
                        P.op('pe', R.transpose(out=ps_[:, j * 128:(j + 1) * 128], in_=xa[:, kc * 128:(kc + 1) * 128], identity=ident),
                             reads=[r_xa, r_cs], writes=[rps])
                    P.op(evac_engine(), (R.tensor_copy(out=mT[:, k4 * 4:k4 * 4 + 4, tt * 128:(tt + 1) * 128], in_=ps_[:].rearrange("p (a b) -> p a b", b=128))) if rr[0] % 2 == 0 else
                         (R.activation(out=mT[:, k4 * 4:k4 * 4 + 4, tt * 128:(tt + 1) * 128], in_=ps_[:].rearrange("p (a b) -> p a b", b=128), func=AF.Copy)),
                         reads=[rps], writes=[r_mT])
            stg = P.sb([128, 512], F32, st); r_stg = P.res()
            if 'mem_ld' in PH or 'mem_fm' in PH or 'mem_tm' in PH:
                Wt, rW = load_w(w_mk[l], 16)
            for cc in range(4 if 'mem_fm' in PH else 0):
                ps_, rps = next_pm()
                for kc in range(16):
                    P.op('pe', R.matmul(ps_[:, 0:256], lhsT=Wt[:, kc, cc * 128:(cc + 1) * 128], rhs=mT[:, kc, :], start=(kc == 0), stop=(kc == 15)),
                         reads=[rW, r_mT], writes=[rps])
                P.op('dve', R.tensor_copy(out=kTm[:, cc, :], in_=ps_[:, 0:256]), reads=[rps], writes=[r_mem])
            for (wsrc, dst, is_v) in (((None, nmk_p, False), (w_mv[l], nmv_p, True)) if 'mem_tm' in PH else ()):
                if wsrc is not None:
                    Wt, rW = load_w(wsrc, 16)
                for tt in range(2):
                    ps_, rps = next_pm()
                    for kc in range(16):
                        P.op('pe', R.matmul(ps_[:], lhsT=mT[:, kc, tt * 128:(tt + 1) * 128], rhs=Wt[:, kc, :], start=(kc == 0), stop=(kc == 15)),
                             reads=[rW, r_mT], writes=[rps])
                    if 'd1' in PH:
                        P.op('act', R.activation(out=stg[:], in_=ps_[:], func=AF.Copy), reads=[rps], writes=[r_stg])
                    if is_v and 'd2' in PH:
                        P.op('dve', R.tensor_copy(out=vm[:, tt, :], in_=ps_[:]), reads=[rps], writes=[r_mem])
                    if 'd3' in PH:
                        P.dma('sp', R.dma_start(out=dst[l, tt * 128:(tt + 1) * 128, :], in_=stg[:]), reads=[r_stg], writes=[r_out])
            P.barrier()

        for ps_i in range(NPASS + 1):
            sample = (ps_i == NPASS)
            if (sample and 'sample' not in PH) or (not sample and 'prompt' not in PH):
                continue
            T = NS if sample else TP
            NTT = 1 if sample else 8
            TW = NS if sample else 128
            NTB = 1 if sample else 2
            TBW = NS if sample else 512
            t_base = 0 if sample else ps_i * TP
            x_in = x_in_s if sample else x_in_p
            r_xin = r_xins if sample else r_xinp
            y_out = y_out_s if sample else y_out_p
            r_y = r_ys if sample else r_yp
            P.barrier()

            with ExitStack() as st:
                xa = [P.sb([128, D], F32, st) for _ in range(2)]; r_xa = [P.res() for _ in range(2)]
                for tt in range(NTT):
                    a = tt % 2
                    P.dma('sp', R.dma_start(out=xa[a][0:TW, :], in_=x_in[t_base + tt * TW:t_base + (tt + 1) * TW, :]),
                          reads=[r_xin], writes=[r_xa[a]])
                    for k4 in range(4):
                        ps_, rps = next_pm()
                        for j in range(4):
                            kc = k4 * 4 + j
                            P.op('pe', R.transpose(out=ps_[:, j * 128:j * 128 + TW], in_=xa[a][0:TW, kc * 128:(kc + 1) * 128], identity=cs[0:TW, C_ID:C_ID + TW]),
                                 reads=[r_xa[a], r_cs], writes=[rps])
                        src = lambda ps_: ps_[:].rearrange("p (a b) -> p a b", b=128)[:, :, 0:TW]
                        if k4 % 2 == 0:
                            P.op('dve', R.tensor_copy(out=xT[:, k4 * 4:k4 * 4 + 4, tt * TW:(tt + 1) * TW], in_=src(ps_)),
                                 reads=[rps], writes=[r_xT])
                        else:
                            P.op('act', R.activation(out=xT[:, k4 * 4:k4 * 4 + 4, tt * TW:(tt + 1) * TW], in_=src(ps_), func=AF.Copy),
                                 reads=[rps], writes=[r_xT])
                P.barrier()

            def proj_fm(col0, ncc, epi, wsrc=None, nk=16, rhsT=None, r_rhs=None, Wt=None, rW=None):
                if Wt is None:
                    Wt, rW = load_w(win[:, col0:col0 + ncc * 128] if wsrc is None else wsrc, nk, ncc * 128)
                rhsT_ = xT if rhsT is None else rhsT
                r_rhs_ = r_xT if r_rhs is None else r_rhs
                for cc in range(ncc):
                    for tb in range(NTB):
                        ps_, rps = next_pm()
                        for kc in range(nk):
                            P.op('pe', R.matmul(ps_[:, 0:TBW], lhsT=Wt[:, kc, cc * 128:(cc + 1) * 128], rhs=rhsT_[:, kc, tb * 512:tb * 512 + TBW], start=(kc == 0), stop=(kc == nk - 1)),
                                 reads=[rW, r_rhs_], writes=[rps])
                        epi(cc, tb, ps_, rps)

            def proj_tm(col0, ncols, epi, wsrc=None, lhs=None, r_lhs=None, nk=16):
                Wt, rW = load_w(win[:, col0:col0 + ncols] if wsrc is None else wsrc, nk, ncols)
                lhs_ = xT if lhs is None else lhs
                r_lhs_ = r_xT if r_lhs is None else r_lhs
                for tt in range(NTT):
                    ps_, rps = next_pm()
                    for kc in range(nk):
                        P.op('pe', R.matmul(ps_[0:TW, 0:ncols], lhsT=lhs_[:, kc, tt * TW:(tt + 1) * TW], rhs=Wt[:, kc, 0:ncols], start=(kc == 0), stop=(kc == nk - 1)),
                             reads=[rW, r_lhs_], writes=[rps])
                    epi(tt, ps_, rps)

            with ExitStack() as st:
              if 'A' in PH:
                vln = P.sb([128, 8, 512], BF16, st); r_vln = P.res()
                lnv = P.sb([128, 2, 512], F32, st); r_lnv = P.res()
                with nc.allow_non_contiguous_dma(reason="param broadcast"):
                    P.dma('sp', R.dma_start(out=lnv[:, 0, :], in_=ln_v_g[l].partition_broadcast(128)), writes=[r_lnv])
                    P.dma('sp', R.dma_start(out=lnv[:, 1, :], in_=ln_v_b[l].partition_broadcast(128)), writes=[r_lnv])
                stg = [P.sb([128, 512], F32, st) for _ in range(2)]; r_stg = [P.res() for _ in range(2)]
                gl = P.sb([128, 512], F32, st); r_gl = P.res()
                stt = P.sb([128, 8], F32, st); r_stt = P.res()
                sc = [0]
                gtmp = (P.sb([128, 512], F32, st), P.sb([128, 512], F32, st), P.res())
                sgA = P.sb([128, 512], F32, st); r_sgA = P.res()

                def epi_va(tt, ps_, rps):
                    gelu_to(gtmp, gl[0:TW, :], ps_[0:TW, :], 512, TW, st, [rps], [r_gl])
                    P.op('dve', R.bn_stats(out=stt[0:TW, 0:6], in_=gl[0:TW, :]), reads=[r_gl], writes=[r_stt])
                    P.op('dve', R.bn_aggr(out=stt[0:TW, 6:8], in_=stt[0:TW, 0:6]), reads=[r_stt], writes=[r_stt])
                    P.op('dve', R.tensor_scalar(out=stt[0:TW, 7:8], in0=stt[0:TW, 7:8], scalar1=EPS, scalar2=None, op0=ALU.add), reads=[r_stt], writes=[r_stt])
                    P.op('act', R.activation(out=stt[0:TW, 7:8], in_=stt[0:TW, 7:8], func=AF.Sqrt), reads=[r_stt], writes=[r_stt])
                    P.op('dve', R.reciprocal(out=stt[0:TW, 7:8], in_=stt[0:TW, 7:8]), reads=[r_stt], writes=[r_stt])
                    P.op('dve', R.tensor_scalar(out=gl[0:TW, :], in0=gl[0:TW, :], scalar1=stt[0:TW, 6:7], scalar2=stt[0:TW, 7:8], op0=ALU.subtract, op1=ALU.mult),
                         reads=[r_gl, r_stt], writes=[r_gl])
                    P.op('dve', R.tensor_tensor(out=gl[0:TW, :], in0=gl[0:TW, :], in1=lnv[0:TW, 0, :], op=ALU.mult), reads=[r_gl, r_lnv], writes=[r_gl])
                    i = sc[0] % 2; sc[0] += 1
                    P.op('dve', R.tensor_tensor(out=stg[i][0:TW, :], in0=gl[0:TW, :], in1=lnv[0:TW, 1, :], op=ALU.add), reads=[r_gl, r_lnv], writes=[r_stg[i]])
                    P.op('act', R.activation(out=vln[0:TW, tt, :], in_=stg[i][0:TW, :], func=AF.Copy), reads=[r_stg[i]], writes=[r_vln])
                    if sample:
                        P.dma('sp', R.dma_start(out=nch_s[l, :, :], in_=stg[i][0:TW, :]), reads=[r_stg[i]], writes=[r_out])
                proj_tm(OFF['v_a'], 512, epi_va)

                ug = P.sb([128, 512], F32, st); r_ug = P.res()

                def epi_ua(cc, tb, ps_, rps):
                    gelu_to(gtmp, ug[:, 0:TBW], ps_[:, 0:TBW], TBW, 128, st, [rps], [r_ug])
                    pa_, rpa = next_pa()
                    ntile = 1 if sample else 4
                    for q in range(ntile):
                        tt = tb * 4 + q
                        if sample:
                            P.op('pe', R.matmul(pa_[:, 0:TW], lhsT=vln[0:TW, 0, cc * 128:(cc + 1) * 128], rhs=wsTs[:, cc, :], start=True, stop=True),
                                 reads=[r_vln, r_prm], writes=[rpa])
                        else:
                            P.op('pe', R.matmul(pa_[:, q * 128:(q + 1) * 128], lhsT=vln[:, tt, cc * 128:(cc + 1) * 128], rhs=wsT[:, cc, :], start=True, stop=True),
                                 reads=[r_vln, r_prm], writes=[rpa])
                    if sample:
                        sg = sgA[:, 0:16]; r_sg = r_sgA
                        P.op('dve', R.tensor_tensor(out=sg, in0=pa_[:, 0:TW], in1=bss[:, cc, :], op=ALU.add), reads=[rpa, r_prm], writes=[r_sg])
                        P.op('dve', R.tensor_tensor(out=bo[0][:, cc, 0:TW], in0=sg, in1=ug[:, 0:TW], op=ALU.mult), reads=[r_sg, r_ug], writes=[r_bo[0]])
                    else:
                        sg = sgA; r_sg = r_sgA
                        P.op('dve', R.tensor_tensor(out=sg[:].rearrange("p (a b) -> p a b", b=128), in0=pa_[:].rearrange("p (a b) -> p a b", b=128),
                                                                      in1=bc(bsb, bsb[:, cc, :].offset, [[512, 128], [0, 4], [1, 128]]), op=ALU.add), reads=[rpa, r_prm], writes=[r_sg])
                        P.op('dve', R.tensor_tensor(out=bo[0][:, cc, tb * 512:(tb + 1) * 512], in0=sg[:], in1=ug[:], op=ALU.mult), reads=[r_sg, r_ug], writes=[r_bo[0]])
                proj_fm(OFF['u_a'], 4, epi_ua)

                def mk_epi_gate(i_bo, st_):
                    sl = P.sb([128, 512], BF16, st_); r_sl = P.res()

                    def epi(cc, tb, ps_, rps):
                        P.op('act', R.activation(out=sl[:, 0:TBW], in_=ps_[:, 0:TBW], func=AF.Silu), reads=[rps], writes=[r_sl])
                        P.op('dve', R.tensor_tensor(out=bo[i_bo][:, cc, tb * 512:tb * 512 + TBW], in0=bo[i_bo][:, cc, tb * 512:tb * 512 + TBW], in1=sl[:, 0:TBW], op=ALU.mult),
                             reads=[r_sl, r_bo[i_bo]], writes=[r_bo[i_bo]])
                    return epi
                proj_fm(OFF['g_a'], 4, mk_epi_gate(0, st))
                P.barrier()

            with ExitStack() as st:
              if 'B' in PH:
                HPW = 136 if sample else 30 + TP
                TA = NS if sample else TP
                hp = P.sb([128, 4, HPW], F32, st); r_hp = P.res()
                yv = P.sb([128, 4, TA], F32, st); r_yv = P.res()
                if sample:
                    sct = P.sb([128, 512], F32, st); r_sct = P.res()
                    for b4 in range(4):
                        P.dma('sp', R.dma_start(out=sct[0:30, :], in_=sconv[l, b4, :, :]), writes=[r_sct])
                        ps_, rps = next_pm()
                        for cc in range(4):
                            P.op('pe', R.transpose(out=ps_[:, cc * 32:cc * 32 + 30], in_=sct[0:30, cc * 128:(cc + 1) * 128], identity=cs[0:30, C_ID:C_ID + 30]),
                                 reads=[r_sct, r_cs], writes=[rps])
                        P.op('dve', R.tensor_copy(out=hp[:, :, b4 * 34:b4 * 34 + 30], in_=ps_[:, 0:128].rearrange("p (a b) -> p a b", b=32)[:, :, 0:30]),
                             reads=[rps], writes=[r_hp])
                        P.dma('sp', R.dma_start(out=ncv_s[l, b4, 0:26, :], in_=sconv[l, b4, 4:30, :]), writes=[r_out])
                else:
                    if ps_i == 0:
                        P.op('dve', R.memset(hp[:, :, 0:30], 0.0), writes=[r_hp])
                    else:
                        P.op('dve', R.tensor_copy(out=hp[:, :, 0:30], in_=halo[:]), reads=[r_halo], writes=[r_hp])

                def hp_dst(cc, tb):
                    if sample:
                        return bc(hp, hp[:, cc, 30:31].offset, [[4 * HPW, 128], [34, 4], [1, 4]])
                    return hp[:, cc, 30 + tb * 512:30 + (tb + 1) * 512]

                def ps_src(ps_):
                    if sample:
                        return ps_[:, 0:16].rearrange("p (a b) -> p a b", b=4)
                    return ps_[:, 0:512]

                def epi_bb(cc, tb, ps_, rps):
                    P.op('act', R.activation(out=hp_dst(cc, tb), in_=ps_src(ps_), func=AF.Sigmoid), reads=[rps], writes=[r_hp])
                proj_fm(OFF['b_b'], 4, epi_bb)

                def epi_ab(cc, tb, ps_, rps):
                    P.op('dve', R.tensor_tensor(out=hp_dst(cc, tb), in0=hp_dst(cc, tb), in1=ps_src(ps_), op=ALU.mult), reads=[rps, r_hp], writes=[r_hp])
                proj_fm(OFF['a_b'], 4, epi_ab)

                def hp_win(cc, j):
                    if sample:
                        return bc(hp, hp[:, cc, j:j + 1].offset, [[4 * HPW, 128], [34, 4], [1, 4]])
                    return hp[:, cc, j:j + TP]

                def yv_ap(cc):
                    if sample:
                        return yv[:, cc, 0:16].rearrange("p (a b) -> p a b", b=4)
                    return yv[:, cc, :]
                for cc in range(4):
                    eng = 'dve'
                    P.op(eng, R.tensor_scalar(out=yv_ap(cc), in0=hp_win(cc, 0), scalar1=wdw[:, cc, 0:1], scalar2=prm[:, cc, 3:4], op0=ALU.mult, op1=ALU.add),
                         reads=[r_hp, r_prm], writes=[r_yv])
                    for j in range(1, 31):
                        P.op(eng, R.scalar_tensor_tensor(out=yv_ap(cc), in0=hp_win(cc, j), scalar=wdw[:, cc, j:j + 1], in1=yv_ap(cc), op0=ALU.mult, op1=ALU.add),
                             reads=[r_hp, r_prm, r_yv], writes=[r_yv])
                if sample:
                    hc = P.sb([128, 4, 16], F32, st); r_hc = P.res()
                    for cc in range(4):
                        P.op('dve', R.tensor_copy(out=hc[:, cc, :].rearrange("p (b t) -> p b t", t=4),
                                                  in_=bc(hp, hp[:, cc, 30:31].offset, [[4 * HPW, 128], [34, 4], [1, 4]])), reads=[r_hp], writes=[r_hc])
                    ps_, rps = next_pm()
                    for cc in range(4):
                        P.op('pe', R.transpose(out=ps_[0:16, cc * 128:(cc + 1) * 128], in_=hc[:, cc, :], identity=ident), reads=[r_hc, r_cs], writes=[rps])
                    hs = P.sb([16, 512], F32, st); r_hs = P.res()
                    P.op('dve', R.tensor_copy(out=hs[:], in_=ps_[0:16, :]), reads=[rps], writes=[r_hs])
                    for b4 in range(4):
                        P.dma('sp', R.dma_start(out=ncv_s[l, b4, 26:30, :], in_=hs[b4 * 4:b4 * 4 + 4, :]), reads=[r_hs], writes=[r_out])
                else:
                    P.op('dve', R.tensor_copy(out=halo[:], in_=hp[:, :, TP:TP + 30]), reads=[r_hp], writes=[r_halo])
                    if ps_i == NPASS - 1:
                        ps_, rps = next_pm()
                        for cc in range(4):
                            P.op('pe', R.transpose(out=ps_[0:30, cc * 128:(cc + 1) * 128], in_=hp[:, cc, TP:TP + 30], identity=ident), reads=[r_hp, r_cs], writes=[rps])
                        hs = P.sb([30, 512], F32, st); r_hs = P.res()
                        P.op('dve', R.tensor_copy(out=hs[:], in_=ps_[0:30, :]), reads=[rps], writes=[r_hs])
                        P.dma('sp', R.dma_start(out=ncv_p[l, :, :], in_=hs[:]), reads=[r_hs], writes=[r_out])
                ysq = [P.sb([128, 512], F32, st) for _ in range(2)]; r_ysq = [P.res() for _ in range(2)]
                zT = P.sb([128, 4, TA], BF16, st); r_zT = P.res()
                mean = P.sb([128, 512], F32, st); rstd = P.sb([128, 512], F32, st); r_ms = P.res()
                for tb in range(NTB):
                    p1, rp1 = next_pm(); p2, rp2 = next_pm()
                    for cc in range(4):
                        P.op('pe', R.matmul(p1[:, 0:TBW], lhsT=onef, rhs=yv[:, cc, tb * 512:tb * 512 + TBW], start=(cc == 0), stop=(cc == 3)), reads=[r_yv, r_cs], writes=[rp1])
                    for cc in range(4):
                        P.op('act', R.activation(out=ysq[cc % 2][:, 0:TBW], in_=yv[:, cc, tb * 512:tb * 512 + TBW], func=AF.Square), reads=[r_yv], writes=[r_ysq[cc % 2]])
                        P.op('pe', R.matmul(p2[:, 0:TBW], lhsT=onef, rhs=ysq[cc % 2][:, 0:TBW], start=(cc == 0), stop=(cc == 3)), reads=[r_ysq[cc % 2], r_cs], writes=[rp2])
                    P.op('act', R.activation(out=mean[:, 0:TBW], in_=p1[:, 0:TBW], func=AF.Copy, scale=1.0 / 512), reads=[rp1], writes=[r_ms])
                    P.op('dve', R.tensor_tensor(out=rstd[:, 0:TBW], in0=mean[:, 0:TBW], in1=mean[:, 0:TBW], op=ALU.mult), reads=[r_ms], writes=[r_ms])
                    P.op('dve', R.scalar_tensor_tensor(out=rstd[:, 0:TBW], in0=p2[:, 0:TBW], scalar=1.0 / 512, in1=rstd[:, 0:TBW], op0=ALU.mult, op1=ALU.subtract), reads=[rp2, r_ms], writes=[r_ms])
                    P.op('dve', R.tensor_scalar(out=rstd[:, 0:TBW], in0=rstd[:, 0:TBW], scalar1=EPS, scalar2=None, op0=ALU.add), reads=[r_ms], writes=[r_ms])
                    P.op('act', R.activation(out=rstd[:, 0:TBW], in_=rstd[:, 0:TBW], func=AF.Sqrt), reads=[r_ms], writes=[r_ms])
                    P.op('dve', R.reciprocal(out=rstd[:, 0:TBW], in_=rstd[:, 0:TBW]), reads=[r_ms], writes=[r_ms])
                    for cc in range(4):
                        ysl = yv[:, cc, tb * 512:tb * 512 + TBW]
                        P.op('dve', R.tensor_tensor(out=ysl, in0=ysl, in1=mean[:, 0:TBW], op=ALU.subtract), reads=[r_yv, r_ms], writes=[r_yv])
                        P.op('dve', R.tensor_tensor(out=ysl, in0=ysl, in1=rstd[:, 0:TBW], op=ALU.mult), reads=[r_yv, r_ms], writes=[r_yv])
                        P.op('act', R.activation(out=zT[:, cc, tb * 512:tb * 512 + TBW], in_=ysl, func=AF.Silu, scale=prm[:, cc, 0:1], bias=prm[:, cc, 1:2]),
                             reads=[r_yv, r_prm], writes=[r_zT])
                sgb = P.sb([128, 4, TA], BF16, st); r_sgb = P.res()

                def epi_gb(cc, tb, ps_, rps):
                    P.op('act', R.activation(out=sgb[:, cc, tb * 512:tb * 512 + TBW], in_=ps_[:, 0:TBW], func=AF.Silu), reads=[rps], writes=[r_sgb])
                proj_fm(OFF['g_b'], 4, epi_gb)

                def epi_pw(cc, tb, ps_, rps):
                    P.op('dve', R.scalar_tensor_tensor(out=bo[1][:, cc, tb * 512:tb * 512 + TBW], in0=ps_[:, 0:TBW], scalar=prm[:, cc, 2:3], in1=sgb[:, cc, tb * 512:tb * 512 + TBW], op0=ALU.add, op1=ALU.mult),
                         reads=[rps, r_prm, r_sgb], writes=[r_bo[1]])
                proj_fm(0, 4, epi_pw, nk=4, rhsT=zT, r_rhs=r_zT, Wt=wpw, rW=r_prm)
                P.barrier()

            with ExitStack() as st:
              if 'C' in PH:
                qT = P.sb([128, 4, TP], BF16, st); r_qT = P.res()
                lf = P.sb([128, 8, 4], F32, st); r_lf = P.res()
                tmp4 = P.sb([128, 8, 4], F32, st); r_t4 = P.res()
                offs = P.sb([128, 9, 4], F32, st); r_offs = P.res()
                tot = P.sb([128, 8, 4], F32, st)
                bias = P.sb([128, 2, 32, 4], F32, st); r_bias = P.res()
                st1 = ExitStack()
                st.enter_context(st1)
                kTl = P.sb([128, 4, TP], BF16, st1); r_kTl = P.res()
                stg = [P.sb([128, 512], F32, st1) for _ in range(2)]; r_stg = [P.res() for _ in range(2)]
                vb = P.sb([128, 8, 512], BF16, st1); r_vb = P.res()
                sc = [0]

                def epi_q(cc, tb, ps_, rps):
                    P.op('act', R.activation(out=qT[:, cc, tb * 512:tb * 512 + TBW], in_=ps_[:, 0:TBW], func=AF.Copy), reads=[rps], writes=[r_qT])
                proj_fm(OFF['q_c'], 4, epi_q)

                def epi_k(cc, tb, ps_, rps):
                    P.op('dve', R.tensor_copy(out=kTl[:, cc, tb * 512:tb * 512 + TBW], in_=ps_[:, 0:TBW]), reads=[rps], writes=[r_kTl])
                proj_fm(OFF['k_c'], 4, epi_k)
                if not sample:
                    for cc in range(4):
                        P.dma('sp', R.dma_start(out=kT_h[cc * 128:(cc + 1) * 128, t_base:t_base + TP], in_=kTl[:, cc, :]), reads=[r_kTl], writes=[r_kTh])

                def mk_epi_kv(dst_p, dst_s, is_v):
                    def epi(tt, ps_, rps):
                        i = sc[0] % 2; sc[0] += 1
                        P.op('act', R.activation(out=stg[i][0:TW, :], in_=ps_[0:TW, :], func=AF.Copy), reads=[rps], writes=[r_stg[i]])
                        if is_v:
                            P.op('dve', R.tensor_copy(out=vb[0:TW, tt, :], in_=ps_[0:TW, :]), reads=[rps], writes=[r_vb])
                        if sample:
                            P.dma('sp', R.dma_start(out=dst_s[l, :, :], in_=stg[i][0:TW, :]), reads=[r_stg[i]], writes=[r_out])
                        else:
                            P.dma('sp', R.dma_start(out=dst_p[l, t_base + tt * 128:t_base + (tt + 1) * 128, :], in_=stg[i][:, :]), reads=[r_stg[i]], writes=[r_out])
                    return epi
                proj_tm(OFF['k_c'], 512, mk_epi_kv(nk_p, nk_s, False))
                proj_tm(OFF['v_c'], 512, mk_epi_kv(nv_p, nv_s, True))
                if not sample:
                    P.dma('sp', R.dma_start(out=v_h[t_base:t_base + TP, :].rearrange("(a p) c -> p a c", p=128), in_=vb[:]), reads=[r_vb], writes=[r_vh])


                def epi_f(tt, ps_, rps):
                    P.op('dve', R.tensor_tensor(out=lf[0:TW, tt, :], in0=ps_[0:TW, 0:4], in1=bfb[0:TW, :], op=ALU.add), reads=[rps, r_prm], writes=[r_lf])
                proj_tm(OFF['f_c'], 4, epi_f)
                lfv = lf[0:TW, 0:NTT, :]; t4v = tmp4[0:TW, 0:NTT, :]
                P.op('act', R.activation(out=t4v, in_=lfv, func=AF.Abs), reads=[r_lf], writes=[r_t4])
                P.op('act', R.activation(out=t4v, in_=t4v, func=AF.Exp, scale=-1.0), reads=[r_t4], writes=[r_t4])
                P.op('act', R.activation(out=t4v, in_=t4v, func=AF.Ln, bias=1.0), reads=[r_t4], writes=[r_t4])
                P.op('dve', R.tensor_scalar(out=lfv, in0=lfv, scalar1=0.0, scalar2=None, op0=ALU.min), reads=[r_lf], writes=[r_lf])
                P.op('dve', R.tensor_tensor(out=lfv, in0=lfv, in1=t4v, op=ALU.subtract), reads=[r_lf, r_t4], writes=[r_lf])
                with nc.allow_non_contiguous_dma(reason="logf rows are 16B"):
                    if sample:
                        P.dma('sp', R.dma_start(out=nlf_s[l, :, :], in_=lf[0:TW, 0, :]), reads=[r_lf], writes=[r_out])
                    else:
                        P.dma('sp', R.dma_start(out=nlf_p[l, t_base:t_base + TP, :].rearrange("(a p) h -> p a h", p=128), in_=lf[:]), reads=[r_lf], writes=[r_out])

                if not sample:
                    pa_, rpa = next_pa()
                    P.op('pe', R.matmul(pa_[:, 0:32], lhsT=tri, rhs=lf[:].rearrange("p a h -> p (a h)"), start=True, stop=True), reads=[r_lf, r_cs], writes=[rpa])
                    P.op('pe', R.matmul(pa_[:, 32:64], lhsT=onef, rhs=lf[:].rearrange("p a h -> p (a h)"), start=True, stop=True), reads=[r_lf, r_cs], writes=[rpa])
                    P.op('dve', R.tensor_copy(out=tot[:].rearrange("p a h -> p (a h)"), in_=pa_[:, 32:64]), reads=[rpa], writes=[r_offs])
                    if ps_i == 0:
                        P.op('dve', R.memset(offs[:, 0, :], 0.0), writes=[r_offs])
                    else:
                        P.op('dve', R.tensor_copy(out=offs[:, 0, :], in_=carry[:]), reads=[r_carry], writes=[r_offs])
                    for tt in range(8):
                        P.op('dve', R.tensor_tensor(out=offs[:, tt + 1, :], in0=offs[:, tt, :], in1=tot[:, tt, :], op=ALU.add), reads=[r_offs], writes=[r_offs])
                    P.op('dve', R.tensor_copy(out=carry[:], in_=offs[:, 8, :]), reads=[r_offs], writes=[r_carry])
                    P.op('dve', R.tensor_tensor(out=ckh[:, ps_i * 8:(ps_i + 1) * 8, :], in0=pa_[:, 0:32].rearrange("p (a h) -> p a h", h=4), in1=offs[:, 0:8, :], op=ALU.add),
                         reads=[rpa, r_offs], writes=[r_ckh])
                    nkb_all = (ps_i + 1) * 8
                    for sb in range(2):
                        P.op('dve', R.tensor_tensor(out=bias[:, sb, 0:nkb_all, :], in0=bc(offs, offs[:, sb * 4, :].offset, [[36, 128], [0, nkb_all], [1, 4]]),
                                                                     in1=ckh[:, 0:nkb_all, :], op=ALU.subtract), reads=[r_offs, r_ckh], writes=[r_bias])
                    P.barrier()
                    st1.close()
                    kTs = [P.sb([128, SEQ], BF16, st) for _ in range(2)]; r_kTs = [P.res() for _ in range(2)]
                    vs = [P.sb([128, 32, 128], BF16, st) for _ in range(2)]; r_vs = [P.res() for _ in range(2)]
                    pt_ = [P.sb([128, 512], BF16, st) for _ in range(3)]; r_pt = [P.res() for _ in range(3)]
                    rs = P.sb([128, 512], F32, st); r_rs = P.res()
                    pc = [0]
                    for h in range(4):
                        a = h % 2
                        nk_tok = nkb_all * 128
                        P.dma('sp', R.dma_start(out=kTs[a][:, 0:nk_tok], in_=kT_h[h * 128:(h + 1) * 128, 0:nk_tok]), reads=[r_kTh], writes=[r_kTs[a]])
                        with nc.allow_non_contiguous_dma(reason="V head slices 256B"):
                            P.dma('sp', R.dma_start(out=vs[a][:, 0:nkb_all, :], in_=v_h[0:nk_tok, h * 128:(h + 1) * 128].rearrange("(a p) c -> p a c", p=128)), reads=[r_vh], writes=[r_vs[a]])
                        for sb in range(2):
                            nkb = ps_i * 8 + sb * 4 + 4
                            for kb in range(nkb):
                                pa_, rpa = next_pa()
                                P.op('pe', R.matmul(pa_[:], lhsT=kTs[a][:, kb * 128:(kb + 1) * 128], rhs=qT[:, h, sb * 512:(sb + 1) * 512], start=True, stop=True),
                                     reads=[r_kTs[a], r_qT], writes=[rpa])
                                i = pc[0] % 3; pc[0] += 1
                                P.op('act', R.activation(out=pt_[i][:], in_=pa_[:], func=AF.Exp, bias=bias[:, sb, kb, h:h + 1], scale=SCALE),
                                     reads=[rpa, r_bias], writes=[r_pt[i]])
                                d = kb - (ps_i * 8 + sb * 4)
                                if d >= 0:
                                    P.op('pool', R.tensor_tensor(out=pt_[i][:], in0=pt_[i][:], in1=maskb[:, d, :], op=ALU.mult), reads=[r_pt[i], r_cb], writes=[r_pt[i]])
                                P.op('pe', R.matmul(po[:], lhsT=vs[a][:, kb, :], rhs=pt_[i][:], start=(kb == 0), stop=(kb == nkb - 1)), reads=[r_vs[a], r_pt[i]], writes=[r_po])
                                P.op('pe', R.matmul(pq[:], lhsT=oneb[:], rhs=pt_[i][:], start=(kb == 0), stop=(kb == nkb - 1)), reads=[r_cb, r_pt[i]], writes=[r_pq])
                            P.op('dve', R.reciprocal(out=rs[:], in_=pq[:]), reads=[r_pq], writes=[r_rs])
                            P.op('dve', R.tensor_tensor(out=bo[2][:, h, sb * 512:(sb + 1) * 512], in0=po[:], in1=rs[:], op=ALU.mult), reads=[r_po, r_rs], writes=[r_bo[2]])
                else:
                    ptb = P.sb([128, 4 * NPG], I32, st); r_ptb = P.res()
                    idx = P.sb([128, 4 * NPG], I32, st)
                    ptT = P.sb([NPG, 4], I32, st); idl = P.sb([NPG, 4], I32, st)
                    with nc.allow_non_contiguous_dma(reason="page table broadcast / transpose"):
                        P.dma('sp', R.dma_start(out=ptb[:], in_=pt.rearrange("b g -> (b g)").partition_broadcast(128)), writes=[r_ptb])
                        P.dma('sp', R.dma_start(out=ptT[:], in_=pt.rearrange("b g -> g b")), writes=[r_ptb])
                    P.op('dve', R.tensor_scalar(out=idx[:], in0=ptb[:], scalar1=128.0, scalar2=float(l * NPOOL * 128), op0=ALU.mult, op1=ALU.add), reads=[r_ptb], writes=[r_ptb])
                    P.op('dve', R.tensor_scalar(out=idx[:], in0=idx[:], scalar1=cs[:, C_IOTA:C_IOTA + 1], scalar2=None, op0=ALU.add), reads=[r_ptb, r_cs], writes=[r_ptb])
                    P.op('dve', R.tensor_scalar(out=idl[:], in0=ptT[:], scalar1=float(l * NPOOL), scalar2=None, op0=ALU.add), reads=[r_ptb], writes=[r_ptb])
                    kpg = [P.sb([128, 512], F32, st) for _ in range(2)]; r_kpg = [P.res() for _ in range(2)]
                    vpg = [P.sb([128, 512], BF16, st) for _ in range(2)]; r_vpg = [P.res() for _ in range(2)]
                    kTp = [P.sb([128, 4, 128], BF16, st) for _ in range(2)]; r_kTp = [P.res() for _ in range(2)]
                    lfp = P.sb([NPG, 128, 4], F32, st); r_lfp = P.res()
                    cp = P.sb([NPG, 128, 4], F32, st); r_cp = P.res()
                    one64 = P.sb([NPG, 128], F32, st)
                    P.op('dve', R.memset(one64[:], 1.0), writes=[r_cp])
                    sfx = P.sb([NPG, 4], F32, st)
                    biasS = P.sb([128, 4, NPG], F32, st); r_bS = P.res()
                    GS = min(32, NPG)
                    sx = P.sb([128, 512], F32, st); r_sx = P.res()
                    pts = P.sb([128, 32, 16], BF16, st); r_pts = P.res()
                    oacc = P.sb([128, 4, 16], F32, st); sacc = P.sb([128, 4, 16], F32, st); r_acc = P.res()
                    red = P.sb([128, 2, 16], F32, st); r_red = P.res()
                    pa_, rpa = next_pa()
                    for h in range(4):
                        P.op('pe', R.matmul(pa_[0:16, h * 16:(h + 1) * 16], lhsT=kTl[:, h, 0:16], rhs=qT[:, h, 0:16], start=True, stop=True), reads=[r_kTl, r_qT], writes=[rpa])
                    P.op('pe', R.matmul(pa_[0:16, 64:68], lhsT=cs[0:16, C_BLKM:C_BLKM + 16], rhs=lf[0:16, 0, :], start=True, stop=True), reads=[r_lf, r_cs], writes=[rpa])
                    cn = P.sb([16, 4], F32, st); r_cn = P.res()
                    P.op('dve', R.tensor_scalar(out=cn[:], in0=pa_[0:16, 64:68], scalar1=-1.0, scalar2=None, op0=ALU.mult), reads=[rpa], writes=[r_cn])
                    pn = P.sb([16, 4, 16], BF16, st); r_pn = P.res()
                    for h in range(4):
                        P.op('act', R.activation(out=pn[:, h, :], in_=pa_[0:16, h * 16:(h + 1) * 16], func=AF.Exp, bias=cn[:, h:h + 1], scale=SCALE), reads=[rpa, r_cn], writes=[r_pn])
                    P.op('dve', R.tensor_tensor(out=pn[:], in0=pn[:], in1=bc(blkmb, 0, [[16, 16], [0, 4], [1, 16]]), op=ALU.mult), reads=[r_pn, r_cb], writes=[r_pn])
                    pa2, rpa2 = next_pa()
                    for h in range(4):
                        P.op('pe', R.matmul(pa2[:, h * 16:(h + 1) * 16], lhsT=vb[0:16, 0, h * 128:(h + 1) * 128], rhs=pn[:, h, :], start=True, stop=True), reads=[r_vb, r_pn], writes=[rpa2])
                    P.op('pe', R.matmul(pa2[:, 64:128], lhsT=oneb[0:16, :], rhs=pn[:].rearrange("p h q -> p (h q)"), start=True, stop=True), reads=[r_cb, r_pn], writes=[rpa2])
                    P.op('dve', R.tensor_copy(out=oacc[:].rearrange("p h q -> p (h q)"), in_=pa2[:, 0:64]), reads=[rpa2], writes=[r_acc])
                    P.op('dve', R.tensor_copy(out=sacc[:].rearrange("p h q -> p (h q)"), in_=pa2[:, 64:128]), reads=[rpa2], writes=[r_acc])
                    gc = [0]
                    for b4 in range(4):
                        P.dma('pool', R.indirect_dma_start(out=lfp[:].rearrange("p k h -> p (k h)"), out_offset=None, in_=cache_lf,
                                                                            in_offset=bass.IndirectOffsetOnAxis(ap=idl[:, b4:b4 + 1], axis=0)), reads=[r_ptb], writes=[r_lfp])
                        for h in range(4):
                            P.op('dve', R.tensor_tensor_scan(out=cp[:, :, h], data0=one64[:], data1=lfp[:, :, h], initial=0.0, op0=ALU.mult, op1=ALU.add), reads=[r_lfp, r_cp], writes=[r_cp])
                        pa_, rpa = next_pa()
                        P.op('pe', R.matmul(pa_[0:NPG, 0:4], lhsT=cs[0:NPG, C_UT:C_UT + NPG], rhs=cp[:, 127, :], start=True, stop=True), reads=[r_cp, r_cs], writes=[rpa])
                        P.op('dve', R.tensor_tensor(out=sfx[:], in0=pa_[0:NPG, 0:4], in1=cp[:, 127, :], op=ALU.add), reads=[rpa, r_cp], writes=[r_cp])
                        P.op('dve', R.tensor_tensor(out=cp[:], in0=bc(sfx, 0, [[4, NPG], [0, 128], [1, 4]]), in1=cp[:], op=ALU.subtract), reads=[r_cp], writes=[r_cp])
                        pa_, rpa = next_pa()
                        for h in range(4):
                            P.op('pe', R.transpose(out=pa_[:, h * NPG:(h + 1) * NPG], in_=cp[:, :, h], identity=cs[0:NPG, C_ID:C_ID + NPG]), reads=[r_cp, r_cs], writes=[rpa])
                        P.op('dve', R.tensor_copy(out=biasS[:].rearrange("p h g -> p (h g)"), in_=pa_[:, 0:4 * NPG]), reads=[rpa], writes=[r_bS])
                        for g0 in range(0, NPG, GS):
                            pS, rpS = next_pa()
                            for g in range(g0, g0 + GS):
                                a = gc[0] % 2; gc[0] += 1
                                col = b4 * NPG + g
                                P.dma('pool', R.indirect_dma_start(out=kpg[a][:], out_offset=None, in_=cache_k,
                                                                                         in_offset=bass.IndirectOffsetOnAxis(ap=idx[:, col:col + 1], axis=0)), reads=[r_ptb], writes=[r_kpg[a]])
                                P.dma('pool', R.indirect_dma_start(out=vpg[a][:], out_offset=None, in_=cache_v,
                                                                                         in_offset=bass.IndirectOffsetOnAxis(ap=idx[:, col:col + 1], axis=0)), reads=[r_ptb, r_pts], writes=[r_vpg[a]])
                                ps_, rps = next_pm()
                                for h in range(4):
                                    P.op('pe', R.transpose(out=ps_[:, h * 128:(h + 1) * 128], in_=kpg[a][:, h * 128:(h + 1) * 128], identity=ident), reads=[r_kpg[a], r_cs], writes=[rps])
                                P.op(('act' if g % 2 else 'dve'), (R.activation(out=kTp[a][:].rearrange("p h k -> p (h k)"), in_=ps_[:], func=AF.Copy)) if g % 2 else
                                     (R.tensor_copy(out=kTp[a][:].rearrange("p h k -> p (h k)"), in_=ps_[:])), reads=[rps], writes=[r_kTp[a]])
                                for h in range(4):
                                    o = (g - g0) * 16 + h * 4
                                    P.op('pe', R.matmul(pS[:, o:o + 4], lhsT=kTp[a][:, h, :], rhs=qT[:, h, b4 * 4:b4 * 4 + 4], start=True, stop=True), reads=[r_kTp[a], r_qT], writes=[rpS])
                            for h in range(4):
                                P.op('dve', R.scalar_tensor_tensor(out=sx[:, 0:GS * 16].rearrange("p (g h q) -> p g h q", h=4, q=4)[:, :, h, :], in0=pS[:, 0:GS * 16].rearrange("p (g h q) -> p g h q", h=4, q=4)[:, :, h, :], scalar=SCALE,
                                                                   in1=bc(biasS, biasS[:, h, g0:g0 + 1].offset, [[4 * NPG, 128], [1, GS], [0, 4]]), op0=ALU.mult, op1=ALU.add), reads=[rpS, r_bS], writes=[r_sx])
                            P.op('act', R.activation(out=pts[:, 0:GS, :].rearrange("p g c -> p (g c)"), in_=sx[:, 0:GS * 16], func=AF.Exp), reads=[r_sx], writes=[r_pts])
                            pov, rpov = next_pm()
                            psv, rpsv = next_pm()
                            for g in range(g0, g0 + GS):
                                a = gc[0] % 2; gc[0] += 1
                                col = b4 * NPG + g
                                P.dma('pool', R.indirect_dma_start(out=vpg[a][:], out_offset=None, in_=cache_v,
                                                                   in_offset=bass.IndirectOffsetOnAxis(ap=idx[:, col:col + 1], axis=0)), reads=[r_ptb], writes=[r_vpg[a]])
                                for h in range(4):
                                    o = (g - g0) * 16 + h * 4
                                    P.op('pe', R.matmul(pov[:, o:o + 4], lhsT=vpg[a][:, h * 128:(h + 1) * 128], rhs=pts[:, g - g0, h * 4:h * 4 + 4], start=True, stop=True),
                                         reads=[r_vpg[a], r_pts], writes=[rpov])
                                o = (g - g0) * 16
                                P.op('pe', R.matmul(psv[:, o:o + 16], lhsT=oneb[:], rhs=pts[:, g - g0, :], start=True, stop=True), reads=[r_cb, r_pts], writes=[rpsv])
                            P.op('dve', R.tensor_reduce(out=red[:, 0, :], in_=pov[:, 0:GS * 16].rearrange("p (g c) -> p c g", c=16), axis=mybir.AxisListType.X, op=ALU.add), reads=[rpov], writes=[r_red])
                            P.op('dve', R.tensor_reduce(out=red[:, 1, :], in_=psv[:, 0:GS * 16].rearrange("p (g c) -> p c g", c=16), axis=mybir.AxisListType.X, op=ALU.add), reads=[rpsv], writes=[r_red])
                            P.op('dve', R.tensor_tensor(out=oacc[:, :, b4 * 4:b4 * 4 + 4], in0=oacc[:, :, b4 * 4:b4 * 4 + 4], in1=red[:, 0, :].rearrange("p (h q) -> p h q", q=4), op=ALU.add), reads=[r_red, r_acc], writes=[r_acc])
                            P.op('dve', R.tensor_tensor(out=sacc[:, :, b4 * 4:b4 * 4 + 4], in0=sacc[:, :, b4 * 4:b4 * 4 + 4], in1=red[:, 1, :].rearrange("p (h q) -> p h q", q=4), op=ALU.add), reads=[r_red, r_acc], writes=[r_acc])
                    P.op('dve', R.reciprocal(out=sacc[:], in_=sacc[:]), reads=[r_acc], writes=[r_acc])
                    P.op('dve', R.tensor_tensor(out=bo[2][:, :, 0:16], in0=oacc[:], in1=sacc[:], op=ALU.mult), reads=[r_acc], writes=[r_bo[2]])
                    P.barrier()
                proj_fm(OFF['g_c'], 4, mk_epi_gate(2, st))
                P.barrier()

            with ExitStack() as st:
              if 'M' in PH:
                qT = P.sb([128, 4, TP], BF16, st); r_qT = P.res()

                def epi_q(cc, tb, ps_, rps):
                    P.op('act', R.activation(out=qT[:, cc, tb * 512:tb * 512 + TBW], in_=ps_[:, 0:TBW], func=AF.Copy), reads=[rps], writes=[r_qT])
                proj_fm(OFF['q_m'], 4, epi_q)
                pt_ = [P.sb([128, 512], BF16, st) for _ in range(3)]; r_pt = [P.res() for _ in range(3)]
                rs = P.sb([128, 512], F32, st); r_rs = P.res()
                pc = [0]
                if not sample:
                    for h in range(4):
                        for sb in range(2):
                            for mb in range(2):
                                pa_, rpa = next_pa()
                                P.op('pe', R.matmul(pa_[:], lhsT=kTm[:, h, mb * 128:(mb + 1) * 128], rhs=qT[:, h, sb * 512:(sb + 1) * 512], start=True, stop=True), reads=[r_mem, r_qT], writes=[rpa])
                                i = pc[0] % 3; pc[0] += 1
                                P.op('act', R.activation(out=pt_[i][:], in_=pa_[:], func=AF.Exp, scale=SCALE), reads=[rpa], writes=[r_pt[i]])
                                P.op('pe', R.matmul(po[:], lhsT=vm[:, mb, h * 128:(h + 1) * 128], rhs=pt_[i][:], start=(mb == 0), stop=(mb == 1)), reads=[r_mem, r_pt[i]], writes=[r_po])
                                P.op('pe', R.matmul(pq[:], lhsT=oneb[:], rhs=pt_[i][:], start=(mb == 0), stop=(mb == 1)), reads=[r_cb, r_pt[i]], writes=[r_pq])
                            P.op('dve', R.reciprocal(out=rs[:], in_=pq[:]), reads=[r_pq], writes=[r_rs])
                            P.op('dve', R.tensor_tensor(out=bo[3][:, h, sb * 512:(sb + 1) * 512], in0=po[:], in1=rs[:], op=ALU.mult), reads=[r_po, r_rs], writes=[r_bo[3]])
                else:
                    mk = P.sb([128, 2, 512], F32, st); r_mk = P.res()
                    mv = P.sb([128, 2, 512], BF16, st); r_mv = P.res()
                    kTs = P.sb([128, 4, 256], BF16, st); r_kTs = P.res()
                    for b4 in range(4):
                        P.dma('sp', R.dma_start(out=mk[:], in_=cmk[l, b4].rearrange("(a p) c -> p a c", p=128)), writes=[r_mk])
                        P.dma('pool', R.dma_start(out=mv[:], in_=cmv[l, b4].rearrange("(a p) c -> p a c", p=128)), writes=[r_mv])
                        for mb in range(2):
                            ps_, rps = next_pm()
                            for h in range(4):
                                P.op('pe', R.transpose(out=ps_[:, h * 128:(h + 1) * 128], in_=mk[:, mb, h * 128:(h + 1) * 128], identity=ident), reads=[r_mk, r_cs], writes=[rps])
                            P.op('dve', R.tensor_copy(out=kTs[:, :, mb * 128:(mb + 1) * 128], in_=ps_[:].rearrange("p (h k) -> p h k", k=128)), reads=[rps], writes=[r_kTs])
                        pa_, rpa = next_pa()
                        for mb in range(2):
                            for h in range(4):
                                o = mb * 16 + h * 4
                                P.op('pe', R.matmul(pa_[:, o:o + 4], lhsT=kTs[:, h, mb * 128:(mb + 1) * 128], rhs=qT[:, h, b4 * 4:b4 * 4 + 4], start=True, stop=True), reads=[r_kTs, r_qT], writes=[rpa])
                        P.op('act', R.activation(out=pt_[0][:, 0:32], in_=pa_[:, 0:32], func=AF.Exp, scale=SCALE), reads=[rpa], writes=[r_pt[0]])
                        pov, rpov = next_pm()
                        for h in range(4):
                            for mb in range(2):
                                o = mb * 16 + h * 4
                                P.op('pe', R.matmul(pov[:, h * 4:h * 4 + 4], lhsT=mv[:, mb, h * 128:(h + 1) * 128], rhs=pt_[0][:, o:o + 4], start=(mb == 0), stop=(mb == 1)), reads=[r_mv, r_pt[0]], writes=[rpov])
                        for mb in range(2):
                            P.op('pe', R.matmul(pov[:, 16:32], lhsT=oneb[:], rhs=pt_[0][:, mb * 16:(mb + 1) * 16], start=(mb == 0), stop=(mb == 1)), reads=[r_cb, r_pt[0]], writes=[rpov])
                        P.op('dve', R.reciprocal(out=rs[:, 0:16], in_=pov[:, 16:32]), reads=[rpov], writes=[r_rs])
                        P.op('dve', R.tensor_tensor(out=bo[3][:, :, b4 * 4:b4 * 4 + 4], in0=pov[:, 0:16].rearrange("p (h q) -> p h q", q=4), in1=rs[:, 0:16].rearrange("p (h q) -> p h q", q=4), op=ALU.mult),
                             reads=[rpov, r_rs], writes=[r_bo[3]])
                proj_fm(OFF['g_m'], 4, mk_epi_gate(3, st))
                P.barrier()

            with ExitStack() as st:
              if 'merge' in PH:
                acc = P.sb([128, 8, 512], F32, st); r_acc = P.res()
                wb = [P.sb([128, 4, 512], BF16, st) for _ in range(2)]; r_wb = [P.res() for _ in range(2)]
                sg = [P.sb([128, 512], F32, st) for _ in range(2)]; r_sg = [P.res() for _ in range(2)]
                wbc = [0]; sgc = [0]
                for mg in range(4):
                    for i in range(4):
                        a = wbc[0] % 2; wbc[0] += 1
                        P.dma('pool', R.dma_start(out=wb[a][:], in_=w_br[l, i, :, mg * 512:(mg + 1) * 512].rearrange("(k p) c -> p k c", p=128)), writes=[r_wb[a]])

                        def epi(cc, tb, ps_, rps, i=i, a=a, mg=mg):
                            pp, rpp = next_pa()
                            for kc in range(4):
                                P.op('pe', R.matmul(pp[:, 0:TBW], lhsT=wb[a][:, kc, cc * 128:(cc + 1) * 128], rhs=bo[i][:, kc, tb * 512:tb * 512 + TBW], start=(kc == 0), stop=(kc == 3)), reads=[r_wb[a], r_bo[i]], writes=[rpp])
                            s = sgc[0] % 2; sgc[0] += 1
                            P.op('act', R.activation(out=sg[s][:, 0:TBW], in_=ps_[:, 0:TBW], func=AF.Sigmoid), reads=[rps], writes=[r_sg[s]])
                            dst = acc[:, cc * 2 + tb, 0:TBW]
                            if i == 0:
                                P.op('dve', R.tensor_tensor(out=dst, in0=sg[s][:, 0:TBW], in1=pp[:, 0:TBW], op=ALU.mult), reads=[r_sg[s], rpp], writes=[r_acc])
                            else:
                                P.op('dve', R.tensor_tensor(out=sg[s][:, 0:TBW], in0=sg[s][:, 0:TBW], in1=pp[:, 0:TBW], op=ALU.mult), reads=[r_sg[s], rpp], writes=[r_sg[s]])
                                P.op('pool', R.tensor_tensor(out=dst, in0=dst, in1=sg[s][:, 0:TBW], op=ALU.add), reads=[r_sg[s], r_acc], writes=[r_acc])
                            if i == 3:
                                P.op('act', R.activation(out=hT[:, mg * 4 + cc, tb * 512:tb * 512 + TBW], in_=dst, func=AF.Copy), reads=[r_acc], writes=[r_hT])
                        proj_fm(OFF['gate'] + i * D + mg * 512, 4, epi)
                P.barrier()

            with ExitStack() as st:
              if 'out' in PH:
                stg = [P.sb([128, 512], F32, st) for _ in range(2)]; r_stg = [P.res() for _ in range(2)]
                xr = [P.sb([128, 512], F32, st) for _ in range(2)]; r_xr = [P.res() for _ in range(2)]
                sc = [0]
                for cg in range(4):
                    def epi(tt, ps_, rps, cg=cg):
                        i = sc[0] % 2; sc[0] += 1
                        P.dma('sp', R.dma_start(out=xr[i][0:TW, :], in_=x_in[t_base + tt * TW:t_base + (tt + 1) * TW, cg * 512:(cg + 1) * 512]), reads=[r_xin], writes=[r_xr[i]])
                        P.op('dve', R.scalar_tensor_tensor(out=stg[i][0:TW, :], in0=xr[i][0:TW, :], scalar=ALPHA, in1=ps_[0:TW, :], op0=ALU.mult, op1=ALU.add), reads=[rps, r_xr[i]], writes=[r_stg[i]])
                        P.dma('sp', R.dma_start(out=ypre[tt * TW:(tt + 1) * TW, cg * 512:(cg + 1) * 512], in_=stg[i][0:TW, :]), reads=[r_stg[i]], writes=[r_ypre])
                    proj_tm(0, 512, epi, wsrc=w_out[l][:, cg * 512:(cg + 1) * 512], lhs=hT, r_lhs=r_hT)
                P.barrier()
            with ExitStack() as st:
              if 'ln' in PH:
                yt = [P.sb([128, D], F32, st) for _ in range(2)]; r_yt = [P.res() for _ in range(2)]
                stt = P.sb([128, 32], F32, st); r_stt = P.res()
                lnf = P.sb([128, 2, D], F32, st); r_lnf = P.res()
                with nc.allow_non_contiguous_dma(reason="param broadcast"):
                    P.dma('sp', R.dma_start(out=lnf[:, 0, :], in_=ln_g[l].partition_broadcast(128)), writes=[r_lnf])
                    P.dma('sp', R.dma_start(out=lnf[:, 1, :], in_=ln_b[l].partition_broadcast(128)), writes=[r_lnf])
                for tt in range(NTT):
                    i = tt % 2
                    P.dma('sp', R.dma_start(out=yt[i][0:TW, :], in_=ypre[tt * TW:(tt + 1) * TW, :]), reads=[r_ypre], writes=[r_yt[i]])
                    for q in range(4):
                        P.op('dve', R.bn_stats(out=stt[0:TW, q * 6:(q + 1) * 6], in_=yt[i][0:TW, q * 512:(q + 1) * 512]), reads=[r_yt[i]], writes=[r_stt])
                    P.op('dve', R.bn_aggr(out=stt[0:TW, 24:26], in_=stt[0:TW, 0:24]), reads=[r_stt], writes=[r_stt])
                    P.op('dve', R.tensor_scalar(out=stt[0:TW, 25:26], in0=stt[0:TW, 25:26], scalar1=EPS, scalar2=None, op0=ALU.add), reads=[r_stt], writes=[r_stt])
                    P.op('act', R.activation(out=stt[0:TW, 25:26], in_=stt[0:TW, 25:26], func=AF.Sqrt), reads=[r_stt], writes=[r_stt])
                    P.op('dve', R.reciprocal(out=stt[0:TW, 25:26], in_=stt[0:TW, 25:26]), reads=[r_stt], writes=[r_stt])
                    P.op('dve', R.tensor_scalar(out=yt[i][0:TW, :], in0=yt[i][0:TW, :], scalar1=stt[0:TW, 24:25], scalar2=stt[0:TW, 25:26], op0=ALU.subtract, op1=ALU.mult), reads=[r_yt[i], r_stt], writes=[r_yt[i]])
                    P.op('pool', R.tensor_tensor(out=yt[i][0:TW, :], in0=yt[i][0:TW, :], in1=lnf[0:TW, 0, :], op=ALU.mult), reads=[r_yt[i], r_lnf], writes=[r_yt[i]])
                    P.op('dve', R.tensor_tensor(out=yt[i][0:TW, :], in0=yt[i][0:TW, :], in1=lnf[0:TW, 1, :], op=ALU.add), reads=[r_yt[i], r_lnf], writes=[r_yt[i]])
                    P.dma('sp', R.dma_start(out=y_out[t_base + tt * TW:t_base + (tt + 1) * TW, :], in_=yt[i][0:TW, :]), reads=[r_yt[i]], writes=[r_y])
                P.barrier()
    nc._dbg = dict(bo0=bo[0].name, bo1=bo[1].name, bo2=bo[2].name, bo3=bo[3].name, hT=hT.name, xT=xT.name, ypre="ypre")
    P.finish()
    return nc


def make_consts():
    c = np.zeros((128, CW), np.float32)
    c[:, C_ID:C_ID + 128] = np.eye(128, dtype=np.float32)
    s = np.arange(128)
    c[:, C_TRI:C_TRI + 128] = (s[:, None] <= s[None, :]).astype(np.float32)
    c[:, C_ONE:C_ONE + 128] = 1.0
    k = np.arange(16)
    c[0:16, C_BLKM:C_BLKM + 16] = ((k[:, None] // 4 == k[None, :] // 4) & (k[:, None] <= k[None, :])).astype(np.float32)
    g = np.arange(64)
    c[0:64, C_UT:C_UT + 64] = (g[:, None] > g[None, :]).astype(np.float32)
    c[:, C_IOTA] = s.astype(np.float32)
    return c


def make_mask():
    s = np.arange(128)
    q = np.arange(512)
    m = np.zeros((128, 2048), np.float32)
    for d in range(4):
        m[:, d * 512:(d + 1) * 512] = ((s[:, None] + 128 * d) <= q[None, :]).astype(np.float32)
    return m


_NC = {}


def kernel(x_prompt, x_sample, mem_prompt, cache_k, cache_v, cache_logf, cache_mem_k, cache_mem_v,
           state_conv, page_table, w_in, w_mem_k, w_mem_v, ln_v_g, ln_v_b, w_s, b_s, w_dw, b_dw,
           ln_c_g, ln_c_b, w_pw, b_pw, b_f, w_branch, w_out, ln_g, ln_b, _PH=ALL_PH):
    f = lambda a: np.ascontiguousarray(np.asarray(a, dtype=np.float32))
    x_prompt = f(x_prompt); x_sample = f(x_sample); mem_prompt = f(mem_prompt)
    depth = int(np.shape(w_in)[0]); seq = int(x_prompt.shape[1]); npool = int(np.shape(cache_k)[1]); npg = int(np.shape(page_table)[1])
    key = (depth, seq // TP, npg, npool, tuple(_PH))
    if key not in _NC:
        _NC[key] = build(DEPTH=depth, NPASS=seq // TP, NPG=npg, NPOOL=npool, PH=_PH)
    nc = _NC[key]
    shared = dict(cache_k=f(cache_k), cache_v=f(cache_v), cache_logf=f(cache_logf), w_in=f(w_in), w_mem_k=f(w_mem_k),
                  w_mem_v=f(w_mem_v), ln_v_g=f(ln_v_g), ln_v_b=f(ln_v_b), w_s=f(w_s), b_s=f(b_s), w_dw=f(w_dw), b_dw=f(b_dw),
                  ln_c_g=f(ln_c_g), ln_c_b=f(ln_c_b), w_pw=f(w_pw), b_pw=f(b_pw), b_f=f(b_f), w_branch=f(w_branch),
                  w_out=f(w_out), ln_g=f(ln_g), ln_b=f(ln_b), cst=make_consts(), cmask=make_mask())
    cmk = f(cache_mem_k).reshape(depth, 32, 256, 512); cmv = f(cache_mem_v).reshape(depth, 32, 256, 512)
    sc = f(state_conv); ptab = np.ascontiguousarray(np.asarray(page_table, dtype=np.int32))
    in_maps = []
    for c in range(8):
        b = c % 2
        m = dict(shared)
        m.update(xp=x_prompt[b], xs=np.ascontiguousarray(x_sample[4 * c:4 * c + 4].reshape(NS, D)), memp=mem_prompt[b],
                 cmk=np.ascontiguousarray(cmk[:, 4 * c:4 * c + 4]), cmv=np.ascontiguousarray(cmv[:, 4 * c:4 * c + 4]),
                 sconv=np.ascontiguousarray(sc[:, 4 * c:4 * c + 4]), pt=np.ascontiguousarray(ptab[4 * c:4 * c + 4]))
        in_maps.append(m)
    res = run_bass_kernel_spmd(nc, in_maps, core_ids=list(range(8))).results
    R_ = lambda name, cores: [res[c][name] for c in cores]
    y_prompt = np.stack(R_("y_p", [0, 1]))
    y_sample = np.concatenate([r.reshape(4, 4, D) for r in R_("y_s", range(8))])
    st2 = lambda name: np.stack(R_(name, [0, 1]), axis=1)
    nk_p = st2("nk_p").reshape(depth, 2, seq, 4, 128); nv_p = st2("nv_p").reshape(depth, 2, seq, 4, 128)
    nlf_p = st2("nlf_p"); ncv_p = st2("ncv_p")
    nmk_p = st2("nmk_p").reshape(depth, 2, 256, 4, 128); nmv_p = st2("nmv_p").reshape(depth, 2, 256, 4, 128)
    cat = lambda name, shp: np.concatenate([r.reshape(shp) for r in R_(name, range(8))], axis=1)
    nk_s = cat("nk_s", (depth, 4, 4, 4, 128)); nv_s = cat("nv_s", (depth, 4, 4, 4, 128))
    nlf_s = cat("nlf_s", (depth, 4, 4, 4)); ncv_s = cat("ncv_s", (depth, 4, 30, BW)); nch_s = cat("nch_s", (depth, 4, 4, BW))
    return (y_prompt, y_sample, nk_p, nv_p, nlf_p, ncv_p, nmk_p, nmv_p, nk_s, nv_s, nlf_s, ncv_s, nch_s)
```

```python
import numpy as np
from contextlib import ExitStack
import concourse.bass as bass
import concourse.mybir as mybir
from concourse.bass_utils import run_bass_kernel_spmd

F32 = mybir.dt.float32
BF16 = mybir.dt.bfloat16
I32 = mybir.dt.int32
AF = mybir.ActivationFunctionType
ALU = mybir.AluOpType

ENG = ['pe', 'act', 'dve', 'pool', 'sp']
NDMA = {'sp': 16, 'pool': 12, 'act': 12}

D = 2048
BW = 512
NCOL = 14340
SEQ = 4096
TP = 1024
NPASS = 4
NS = 16
DEPTH = 2
NPOOL = 2560
NPG = 64
ALPHA = (2 * DEPTH) ** 0.25
EPS = 1e-5
SCALE = 128 ** -0.5
OFF = dict(u_a=0, v_a=512, g_a=1024, a_b=1536, b_b=2048, g_b=2560, q_c=3072, k_c=3584, v_c=4096,
           f_c=4608, g_c=4612, q_m=5124, g_m=5636, gate=6148)
C_ID = 0
C_TRI = 128
C_ONE = 256
C_BLKM = 384
C_UT = 400
C_IOTA = 464
CW = 465


class Res:
    __slots__ = ('name', 'w', 'r', 'excl', 'multi', 'wl')

    def __init__(self, name, excl=False, multi=False):
        self.name = name
        self.w = None
        self.r = {}
        self.excl = excl
        self.multi = multi
        self.wl = {}


class Prog:
    def __init__(self, nc):
        self.nc = nc
        self.es = ExitStack()
        self.streams = {e: [] for e in ENG}
        self.cnt = {e: 0 for e in ENG}
        self.csem = {e: nc.alloc_semaphore('c_' + e) for e in ENG}
        self.known = {e: {} for e in ENG}
        self.dsem = {q: [nc.alloc_semaphore('d_%s_%d' % (q, i)) for i in range(n)] for q, n in NDMA.items()}
        self.dval = {q: [0] * n for q, n in NDMA.items()}
        self.dnext = {q: 0 for q in NDMA}
        self.nres = 0
        self.nt = 0
        self.wsem = [nc.alloc_semaphore('w_%d' % i) for i in range(2)]
        self.wval = [0, 0]
        self.wlast = None

    def sb(self, shape, dt, stack=None):
        self.nt += 1
        return (stack or self.es).enter_context(self.nc.sbuf_tensor('t%d' % self.nt, list(shape), dt))

    def ps(self, shape, dt=F32):
        self.nt += 1
        return self.es.enter_context(self.nc.psum_tensor('p%d' % self.nt, list(shape), dt))

    def res(self, name=None, excl=False, multi=False):
        self.nres += 1
        return Res(name or ('r%d' % self.nres), excl, multi)

    def _sem_of(self, key):
        if key[0] == 'e':
            return self.csem[key[1]]
        if key[0] == 'w':
            return self.wsem[key[1]]
        return self.dsem[key[1]][key[2]]

    def wload(self, slot, fn, res):
        need = {}
        toks = list(res.r.items())
        if res.w is not None:
            toks.append(res.w)
        for key, val in toks:
            if need.get(key, 0) < val:
                need[key] = val
        waits = [(self._sem_of(k), v) for k, v in need.items()]
        self.wval[slot] += 16
        tok = (('w', slot), self.wval[slot])
        res.w = tok
        res.r = {}
        sem = self.wsem[slot]

        def emit(e):
            for s, v in waits:
                e.wait_ge(s, v)
            fn(e).then_inc(sem, 16)
        st = self.streams['pool']
        marker = lambda e: None
        if self.wlast is None:
            st.append(emit)
        else:
            st.insert(st.index(self.wlast) + 1, emit)
        st.append(marker)
        self.wlast = marker

    def _collect(self, eng, reads, writes, extra=()):
        need = {}
        toks = []
        for r in reads:
            if r.multi:
                toks.extend(r.wl.items())
            elif r.w is not None:
                toks.append(r.w)
        for w in writes:
            if w.w is not None and not w.multi:
                toks.append(w.w)
            toks.extend(w.r.items())
        toks.extend(extra)
        kn = self.known[eng]
        for key, val in toks:
            if key == ('e', 'pe') and eng == 'pe':
                continue
            if kn.get(key, 0) >= val:
                continue
            if need.get(key, 0) < val:
                need[key] = val
        for key, val in need.items():
            kn[key] = val
        return [(self._sem_of(k), v) for k, v in need.items()]

    def _mark(self, tok, reads, writes):
        key, val = tok
        for r in reads:
            r.r[key] = val
        for w in writes:
            if w.multi:
                if w.r:
                    w.wl = {}
                w.wl[key] = max(w.wl.get(key, 0), val)
            w.w = tok
            w.r = {}

    def op(self, eng, fn, reads=(), writes=()):
        if any(r.excl for r in reads):
            writes = list(writes) + [r for r in reads if r.excl]
            reads = [r for r in reads if not r.excl]
        waits = self._collect(eng, reads, writes)
        self.cnt[eng] += 1
        tok = (('e', eng), self.cnt[eng])
        self._mark(tok, reads, writes)
        sem = self.csem[eng]

        def emit(e):
            for s, v in waits:
                e.wait_ge(s, v)
            fn(e).then_inc(sem, 1)
        self.streams[eng].append(emit)

    def dma(self, q, fn, reads=(), writes=()):
        n = len(self.dsem[q])
        slot = self.dnext[q] % n
        self.dnext[q] += 1
        prev = self.dval[q][slot]
        key = ('d', q, slot)
        extra = [(key, prev)] if prev > 0 else []
        waits = self._collect(q, reads, writes, extra)
        newv = prev + 16
        self.dval[q][slot] = newv
        self.known[q][key] = max(self.known[q].get(key, 0), prev)
        self._mark((key, newv), reads, writes)
        sem = self.dsem[q][slot]

        def emit(e):
            for s, v in waits:
                e.wait_ge(s, v)
            with self.nc.allow_non_contiguous_dma(reason="small strided param/state transfers"):
                ins = fn(e)
            ins.then_inc(sem, 16)
        self.streams[q].append(emit)

    def barrier(self):
        waits = []
        for q in NDMA:
            for i, v in enumerate(self.dval[q]):
                if v > 0:
                    waits.append((('d', q, i), v))
        for e in ENG:
            if self.cnt[e] > 0:
                waits.append((('e', e), self.cnt[e]))
        for eng in ENG:
            kn = self.known[eng]
            need = []
            for key, v in waits:
                if key == ('e', eng):
                    continue
                if kn.get(key, 0) >= v:
                    continue
                kn[key] = v
                need.append((self._sem_of(key), v))

            def emit(e, need=need):
                for s, v in need:
                    e.wait_ge(s, v)
            self.streams[eng].append(emit)

    def finish(self):
        self.barrier()
        nc = self.nc
        with nc.Block() as block:
            @block.tensor
            def _(e):
                for c in self.streams['pe']:
                    c(e)

            @block.scalar
            def _(e):
                for c in self.streams['act']:
                    c(e)

            @block.vector
            def _(e):
                for c in self.streams['dve']:
                    c(e)

            @block.gpsimd
            def _(e):
                for c in self.streams['pool']:
                    c(e)

            @block.sync
            def _(e):
                for c in self.streams['sp']:
                    c(e)
        self.es.close()


class _Rec:
    def __getattr__(self, name):
        def mk(*a, **k):
            return lambda e: getattr(e, name)(*a, **k)
        return mk


R = _Rec()


def bc(ap_tensor, offset, dims):
    return bass.AP(ap_tensor, offset, [list(d) for d in dims])


ALL_PH = ('params', 'mem', 'mem_fm', 'mem_tm', 'd1', 'd2', 'd3', 'A', 'B', 'C', 'M', 'merge', 'out', 'ln', 'prompt', 'sample')


def build(DEPTH=DEPTH, NPASS=NPASS, NPG=NPG, NPOOL=NPOOL, PH=ALL_PH):
    SEQ = NPASS * TP
    nc = bass.Bass("TRN2", target_bir_lowering=False)
    P = Prog(nc)

    def din(name, shape, dt=F32):
        return nc.dram_tensor(name, list(shape), dt, kind="ExternalInput").ap()

    def dout(name, shape, dt=F32):
        return nc.dram_tensor(name, list(shape), dt, kind="ExternalOutput").ap()

    def dscr(name, shape, dt=F32):
        return nc.dram_tensor(name, list(shape), dt).ap()

    xp = din("xp", [SEQ, D])
    xs = din("xs", [NS, D])
    memp = din("memp", [256, D])
    cache_k = din("cache_k", [DEPTH, NPOOL, 128, 4, 128]).rearrange("l n r h e -> (l n r) (h e)")
    cache_v = din("cache_v", [DEPTH, NPOOL, 128, 4, 128]).rearrange("l n r h e -> (l n r) (h e)")
    cache_lf = din("cache_logf", [DEPTH, NPOOL, 128, 4]).rearrange("l n r h -> (l n) (r h)")
    cmk = din("cmk", [DEPTH, 4, 256, 512])
    cmv = din("cmv", [DEPTH, 4, 256, 512])
    sconv = din("sconv", [DEPTH, 4, 30, 512])
    pt = din("pt", [4, NPG], I32)
    w_in = din("w_in", [DEPTH, D, NCOL])
    w_mk = din("w_mem_k", [DEPTH, D, BW])
    w_mv = din("w_mem_v", [DEPTH, D, BW])
    ln_v_g = din("ln_v_g", [DEPTH, BW]); ln_v_b = din("ln_v_b", [DEPTH, BW])
    w_s = din("w_s", [DEPTH, 4, 128, 128]); b_s = din("b_s", [DEPTH, 4, 128])
    w_dw = din("w_dw", [DEPTH, 31, BW]); b_dw = din("b_dw", [DEPTH, BW])
    ln_c_g = din("ln_c_g", [DEPTH, BW]); ln_c_b = din("ln_c_b", [DEPTH, BW])
    w_pw = din("w_pw", [DEPTH, BW, BW]); b_pw = din("b_pw", [DEPTH, BW])
    b_f = din("b_f", [DEPTH, 4])
    w_br = din("w_branch", [DEPTH, 4, BW, D])
    w_out = din("w_out", [DEPTH, D, D])
    ln_g = din("ln_g", [DEPTH, D]); ln_b = din("ln_b", [DEPTH, D])
    cst = din("cst", [128, CW])
    cmask = din("cmask", [128, 2048])

    y_p = dout("y_p", [SEQ, D]); y_s = dout("y_s", [NS, D])
    nk_p = dout("nk_p", [DEPTH, SEQ, BW]); nv_p = dout("nv_p", [DEPTH, SEQ, BW])
    nlf_p = dout("nlf_p", [DEPTH, SEQ, 4]); ncv_p = dout("ncv_p", [DEPTH, 30, BW])
    nmk_p = dout("nmk_p", [DEPTH, 256, BW]); nmv_p = dout("nmv_p", [DEPTH, 256, BW])
    nk_s = dout("nk_s", [DEPTH, NS, BW]); nv_s = dout("nv_s", [DEPTH, NS, BW])
    nlf_s = dout("nlf_s", [DEPTH, NS, 4]); ncv_s = dout("ncv_s", [DEPTH, 4, 30, BW])
    nch_s = dout("nch_s", [DEPTH, NS, BW])

    x1_p = dscr("x1_p", [SEQ, D]); x1_s = dscr("x1_s", [NS, D])
    ypre = dscr("ypre", [TP, D])
    kT_h = dscr("kT_h", [BW, SEQ], BF16)
    v_h = dscr("v_h", [SEQ, BW], BF16)
    r_x1p = P.res(multi=True); r_x1s = P.res(multi=True); r_ypre = P.res(multi=True); r_kTh = P.res(multi=True); r_vh = P.res(multi=True)
    r_out = P.res(multi=True)

    cs = P.sb([128, CW], F32); r_cs = P.res()
    idb = P.sb([128, 128], BF16)
    oneb = P.sb([128, 128], BF16)
    maskb = P.sb([128, 4, 512], BF16)
    blkmb = P.sb([16, 16], BF16)
    r_cb = P.res()
    xT = P.sb([128, 16, TP], BF16); r_xT = P.res()
    hT = P.sb([128, 16, TP], BF16); r_hT = P.res()
    bo = [P.sb([128, 4, TP], BF16) for _ in range(4)]; r_bo = [P.res() for _ in range(4)]
    W = [P.sb([128, 16, 512], BF16) for _ in range(2)]; r_W = [P.res() for _ in range(2)]
    wctr = [0]
    halo = P.sb([128, 4, 30], F32); r_halo = P.res()
    ckh = P.sb([128, 32, 4], F32); r_ckh = P.res()
    carry = P.sb([128, 4], F32); r_carry = P.res()
    kTm = P.sb([128, 4, 256], BF16); vm = P.sb([128, 2, 512], BF16); r_mem = P.res()
    prm = P.sb([128, 4, 8], F32); r_prm = P.res()
    wdw = P.sb([128, 4, 31], F32)
    bsb = P.sb([128, 4, 128], F32)
    wsT = P.sb([128, 4, 128], BF16)
    wsTs = P.sb([16, 4, 16], BF16)
    bss = P.sb([128, 4, 16], F32)
    bfb = P.sb([128, 4], F32)
    wpw = P.sb([128, 4, 512], BF16)
    pm = [P.ps([128, 512]) for _ in range(4)]; r_pm = [P.res(excl=True) for _ in range(4)]
    pa = [P.ps([128, 512]) for _ in range(2)]; r_pa = [P.res(excl=True) for _ in range(2)]
    po = P.ps([128, 512]); r_po = P.res(excl=True)
    pq = P.ps([128, 512]); r_pq = P.res(excl=True)
    pmc = [0]; pac = [0]

    def next_pm():
        i = pmc[0] % 4; pmc[0] += 1
        return pm[i], r_pm[i]

    def next_pa():
        i = pac[0] % 2; pac[0] += 1
        return pa[i], r_pa[i]

    def load_w(src_ap, nk, ncols=512):
        i = wctr[0] % 2; wctr[0] += 1
        P.wload(i, R.dma_start(out=W[i][:, 0:nk, 0:ncols], in_=src_ap.rearrange("(k p) c -> p k c", p=128)), r_W[i])
        return W[i], r_W[i]

    rr = [0]

    def evac_engine():
        rr[0] += 1
        return 'act' if rr[0] % 2 else 'dve'

    P.dma('sp', R.dma_start(out=cs[:], in_=cst[:, :]), writes=[r_cs])
    P.op('dve', R.tensor_copy(out=idb[:], in_=cs[:, C_ID:C_ID + 128]), reads=[r_cs], writes=[r_cb])
    P.op('dve', R.tensor_copy(out=oneb[:], in_=cs[:, C_ONE:C_ONE + 128]), reads=[r_cs], writes=[r_cb])
    with ExitStack() as st0:
        mtmp = P.sb([128, 2048], F32, st0); r_mtmp = P.res()
        P.dma('sp', R.dma_start(out=mtmp[:], in_=cmask[:, :]), writes=[r_mtmp])
        P.op('dve', R.tensor_copy(out=maskb[:].rearrange("p a b -> p (a b)"), in_=mtmp[:]), reads=[r_mtmp], writes=[r_cb])
        P.barrier()
    P.op('dve', R.tensor_copy(out=blkmb[:], in_=cs[0:16, C_BLKM:C_BLKM + 16]), reads=[r_cs], writes=[r_cb])
    ident = cs[:, C_ID:C_ID + 128]
    tri = cs[:, C_TRI:C_TRI + 128]
    onef = cs[:, C_ONE:C_ONE + 128]

    def gelu_to(tmps, out_ap, ps_ap, n, npart, st, reads, writes):
        t1, t2, rt = tmps
        P.op('act', R.activation(out=t1[0:npart, 0:n], in_=ps_ap, func=AF.Square), reads=reads, writes=[rt])
        P.op('dve', R.tensor_scalar(out=t1[0:npart, 0:n], in0=t1[0:npart, 0:n], scalar1=0.044715, scalar2=1.0,
                                              op0=ALU.mult, op1=ALU.add), reads=[rt], writes=[rt])
        P.op('dve', R.tensor_tensor(out=t1[0:npart, 0:n], in0=t1[0:npart, 0:n], in1=ps_ap, op=ALU.mult),
             reads=[rt] + list(reads), writes=[rt])
        P.op('act', R.activation(out=t2[0:npart, 0:n], in_=t1[0:npart, 0:n], func=AF.Sigmoid, scale=1.5957691216057308),
             reads=[rt], writes=[rt])
        P.op('dve', R.tensor_tensor(out=out_ap, in0=t2[0:npart, 0:n], in1=ps_ap, op=ALU.mult),
             reads=[rt] + list(reads), writes=writes)

    for l in range(DEPTH):
        x_in_p = xp if l == 0 else x1_p
        x_in_s = xs if l == 0 else x1_s
        last = (l == DEPTH - 1)
        y_out_p = y_p if last else x1_p
        y_out_s = y_s if last else x1_s
        r_yp = r_out if last else r_x1p
        r_ys = r_out if last else r_x1s
        r_xinp = r_cs if l == 0 else r_x1p
        r_xins = r_cs if l == 0 else r_x1s
        win = w_in[l]

        P.barrier()
        with ExitStack() as st:
            def fm_param(src, j):
                P.dma('sp', R.dma_start(out=prm[:, :, j:j + 1], in_=src.rearrange("(c p o) -> p c o", p=128, o=1)),
                      writes=[r_prm])
            with nc.allow_non_contiguous_dma(reason="small param loads"):
                fm_param(ln_c_g[l], 0); fm_param(ln_c_b[l], 1); fm_param(b_pw[l], 2); fm_param(b_dw[l], 3)
                for cc in range(4):
                    P.dma('sp', R.dma_start(out=wdw[:, cc, :], in_=w_dw[l][:, cc * 128:(cc + 1) * 128].rearrange("j p -> p j")), writes=[r_prm])
                P.dma('sp', R.dma_start(out=bsb[:], in_=b_s[l].partition_broadcast(128)), writes=[r_prm])
                P.dma('sp', R.dma_start(out=bfb[:], in_=b_f[l].partition_broadcast(128)), writes=[r_prm])
                for g in range(4):
                    P.dma('sp', R.dma_start(out=bss[:, g, :].rearrange("p (b t) -> p b t", t=4),
                                                           in_=bc(b_s.tensor, b_s[l, g, 0:4].offset, [[0, 128], [0, 4], [1, 4]])),
                          writes=[r_prm])
            P.dma('pool', R.dma_start(out=wpw[:], in_=w_pw[l].rearrange("(k p) c -> p k c", p=128)), writes=[r_prm])
            wsf = P.sb([128, 4, 128], F32, st); r_wsf = P.res()
            P.dma('sp', R.dma_start(out=wsf[:], in_=w_s[l].rearrange("g t s -> t g s")), writes=[r_wsf])
            ps_, rps = next_pm()
            for g in range(4):
                P.op('pe', R.transpose(out=ps_[:, g * 128:(g + 1) * 128], in_=wsf[:, g, :], identity=ident),
                     reads=[r_wsf, r_cs], writes=[rps])
            for g in range(4):
                P.op('dve', R.tensor_tensor(out=wsT[:, g, :], in0=ps_[:, g * 128:(g + 1) * 128], in1=tri, op=ALU.mult),
                     reads=[rps, r_cs], writes=[r_prm])
            wss = P.sb([16, 4, 16], F32, st); r_wss = P.res()
            P.op('dve', R.memset(wss[:], 0.0), writes=[r_wss])
            with nc.allow_non_contiguous_dma(reason="tiny 4x4 transposed blocks"):
                for b4 in range(4):
                    for g in range(4):
                        P.dma('sp', R.dma_start(
                            out=wss[b4 * 4:b4 * 4 + 4, g, b4 * 4:b4 * 4 + 4],
                            in_=w_s[l, g, 0:4, 0:4].rearrange("t s -> s t")), writes=[r_wss])
            for g in range(4):
                P.op('dve', R.tensor_tensor(out=wsTs[:, g, :], in0=wss[:, g, :], in1=cs[0:16, C_BLKM:C_BLKM + 16], op=ALU.mult),
                     reads=[r_wss, r_cs], writes=[r_prm])
            P.barrier()

        with ExitStack() as st:
          if 'mem' in PH:
            mT = P.sb([128, 16, 256], BF16, st); r_mT = P.res()
            xa = P.sb([128, D], F32, st); r_xa = P.res()
            for tt in range(2):
                P.dma('sp', R.dma_start(out=xa[:], in_=memp[tt * 128:(tt + 1) * 128, :]), writes=[r_xa])
                for k4 in range(4):
                    ps_, rps = next_pm()
                    for j in range(4):
                        kc = k4 * 4 + j
                        P.op('pe', R.transpose(out=ps_[:, j * 128:(j + 1) * 128], in_=xa[:, kc * 128:(kc + 1) * 128], identity=ident),
                             reads=[r_xa, r_cs], writes=[rps])
                    P.op(evac_engine(), (R.tensor_copy(out=mT[:, k4 * 4:k4 * 4 + 4, tt * 128:(tt + 1) * 128], in_=ps_[:].rearrange("p (a b) -> p a b", b=128))) if rr[0] % 2 == 0 else
                         (R.activation(out=mT[:, k4 * 4:k4 * 4 + 4, tt * 128:(tt + 1) * 128], in_=ps_[:].rearrange("p (a b) -> p a b", b=128), func=AF.Copy)),
                         reads=[rps], writes=[r_mT])
            stg = P.sb([128, 512], F32, st); r_stg = P.res()
            if 'mem_ld' in PH or 'mem_fm' in PH or 'mem_tm' in PH:
                Wt, rW = load_w(w_mk[l], 16)
            for cc in range(4 if 'mem_fm' in PH else 0):
                ps_, rps = next_pm()
                for kc in range(16):
                    P.op('pe', R.matmul(ps_[:, 0:256], lhsT=Wt[:, kc, cc * 128:(cc + 1) * 128], rhs=mT[:, kc, :], start=(kc == 0), stop=(kc == 15)),
                         reads=[rW, r_mT], writes=[rps])
                P.op('dve', R.tensor_copy(out=kTm[:, cc, :], in_=ps_[:, 0:256]), reads=[rps], writes=[r_mem])
            for (wsrc, dst, is_v) in (((None, nmk_p, False), (w_mv[l], nmv_p, True)) if 'mem_tm' in PH else ()):
                if wsrc is not None:
                    Wt, rW = load_w(wsrc, 16)
                for tt in range(2):
                    ps_, rps = next_pm()
                    for kc in range(16):
                        P.op('pe', R.matmul(ps_[:], lhsT=mT[:, kc, tt * 128:(tt + 1) * 128], rhs=Wt[:, kc, :], start=(kc == 0), stop=(kc == 15)),
                             reads=[rW, r_mT], writes=[rps])
                    if 'd1' in PH:
                        P.op('act', R.activation(out=stg[:], in_=ps_[:], func=AF.Copy), reads=[rps], writes=[r_stg])
                    if is_v and 'd2' in PH:
                        P.op('dve', R.tensor_copy(out=vm[:, tt, :], in_=ps_[:]), reads=[rps], writes=[r_mem])
                    if 'd3' in PH:
                        P.dma('act', R.dma_start(out=dst[l, tt * 128:(tt + 1) * 128, :], in_=stg[:]), reads=[r_stg], writes=[r_out])
            P.barrier()

        for ps_i in range(NPASS + 1):
            sample = (ps_i == NPASS)
            if (sample and 'sample' not in PH) or (not sample and 'prompt' not in PH):
                continue
            T = NS if sample else TP
            NTT = 1 if sample else 8
            TW = NS if sample else 128
            NTB = 1 if sample else 2
            TBW = NS if sample else 512
            t_base = 0 if sample else ps_i * TP
            x_in = x_in_s if sample else x_in_p
            r_xin = r_xins if sample else r_xinp
            y_out = y_out_s if sample else y_out_p
            r_y = r_ys if sample else r_yp
            P.barrier()

            with ExitStack() as st:
                xa = [P.sb([128, D], F32, st) for _ in range(2)]; r_xa = [P.res() for _ in range(2)]
                for tt in range(NTT):
                    a = tt % 2
                    P.dma('sp', R.dma_start(out=xa[a][0:TW, :], in_=x_in[t_base + tt * TW:t_base + (tt + 1) * TW, :]),
                          reads=[r_xin], writes=[r_xa[a]])
                    for k4 in range(4):
                        ps_, rps = next_pm()
                        for j in range(4):
                            kc = k4 * 4 + j
                            P.op('pe', R.transpose(out=ps_[:, j * 128:j * 128 + TW], in_=xa[a][0:TW, kc * 128:(kc + 1) * 128], identity=cs[0:TW, C_ID:C_ID + TW]),
                                 reads=[r_xa[a], r_cs], writes=[rps])
                        src = lambda ps_: ps_[:].rearrange("p (a b) -> p a b", b=128)[:, :, 0:TW]
                        if k4 % 2 == 0:
                            P.op('dve', R.tensor_copy(out=xT[:, k4 * 4:k4 * 4 + 4, tt * TW:(tt + 1) * TW], in_=src(ps_)),
                                 reads=[rps], writes=[r_xT])
                        else:
                            P.op('act', R.activation(out=xT[:, k4 * 4:k4 * 4 + 4, tt * TW:(tt + 1) * TW], in_=src(ps_), func=AF.Copy),
                                 reads=[rps], writes=[r_xT])
                P.barrier()

            def proj_fm(col0, ncc, epi, wsrc=None, nk=16, rhsT=None, r_rhs=None, Wt=None, rW=None):
                if Wt is None:
                    Wt, rW = load_w(win[:, col0:col0 + ncc * 128] if wsrc is None else wsrc, nk, ncc * 128)
                rhsT_ = xT if rhsT is None else rhsT
                r_rhs_ = r_xT if r_rhs is None else r_rhs
                for cc in range(ncc):
                    for tb in range(NTB):
                        ps_, rps = next_pm()
                        for kc in range(nk):
                            P.op('pe', R.matmul(ps_[:, 0:TBW], lhsT=Wt[:, kc, cc * 128:(cc + 1) * 128], rhs=rhsT_[:, kc, tb * 512:tb * 512 + TBW], start=(kc == 0), stop=(kc == nk - 1)),
                                 reads=[rW, r_rhs_], writes=[rps])
                        epi(cc, tb, ps_, rps)

            def proj_tm(col0, ncols, epi, wsrc=None, lhs=None, r_lhs=None, nk=16):
                Wt, rW = load_w(win[:, col0:col0 + ncols] if wsrc is None else wsrc, nk, ncols)
                lhs_ = xT if lhs is None else lhs
                r_lhs_ = r_xT if r_lhs is None else r_lhs
                for tt in range(NTT):
                    ps_, rps = next_pm()
                    for kc in range(nk):
                        P.op('pe', R.matmul(ps_[0:TW, 0:ncols], lhsT=lhs_[:, kc, tt * TW:(tt + 1) * TW], rhs=Wt[:, kc, 0:ncols], start=(kc == 0), stop=(kc == nk - 1)),
                             reads=[rW, r_lhs_], writes=[rps])
                    epi(tt, ps_, rps)

            with ExitStack() as st:
              if 'A' in PH:
                vln = P.sb([128, 8, 512], BF16, st); r_vln = P.res()
                lnv = P.sb([128, 2, 512], F32, st); r_lnv = P.res()
                with nc.allow_non_contiguous_dma(reason="param broadcast"):
                    P.dma('sp', R.dma_start(out=lnv[:, 0, :], in_=ln_v_g[l].partition_broadcast(128)), writes=[r_lnv])
                    P.dma('sp', R.dma_start(out=lnv[:, 1, :], in_=ln_v_b[l].partition_broadcast(128)), writes=[r_lnv])
                stg = [P.sb([128, 512], F32, st) for _ in range(2)]; r_stg = [P.res() for _ in range(2)]
                gl = P.sb([128, 512], F32, st); r_gl = P.res()
                stt = P.sb([128, 8], F32, st); r_stt = P.res()
                sc = [0]
                gtmp = (P.sb([128, 512], F32, st), P.sb([128, 512], F32, st), P.res())
                sgA = P.sb([128, 512], F32, st); r_sgA = P.res()

                def epi_va(tt, ps_, rps):
                    gelu_to(gtmp, gl[0:TW, :], ps_[0:TW, :], 512, TW, st, [rps], [r_gl])
                    P.op('dve', R.bn_stats(out=stt[0:TW, 0:6], in_=gl[0:TW, :]), reads=[r_gl], writes=[r_stt])
                    P.op('dve', R.bn_aggr(out=stt[0:TW, 6:8], in_=stt[0:TW, 0:6]), reads=[r_stt], writes=[r_stt])
                    P.op('dve', R.tensor_scalar(out=stt[0:TW, 7:8], in0=stt[0:TW, 7:8], scalar1=EPS, scalar2=None, op0=ALU.add), reads=[r_stt], writes=[r_stt])
                    P.op('act', R.activation(out=stt[0:TW, 7:8], in_=stt[0:TW, 7:8], func=AF.Sqrt), reads=[r_stt], writes=[r_stt])
                    P.op('dve', R.reciprocal(out=stt[0:TW, 7:8], in_=stt[0:TW, 7:8]), reads=[r_stt], writes=[r_stt])
                    P.op('dve', R.tensor_scalar(out=gl[0:TW, :], in0=gl[0:TW, :], scalar1=stt[0:TW, 6:7], scalar2=stt[0:TW, 7:8], op0=ALU.subtract, op1=ALU.mult),
                         reads=[r_gl, r_stt], writes=[r_gl])
                    P.op('dve', R.tensor_tensor(out=gl[0:TW, :], in0=gl[0:TW, :], in1=lnv[0:TW, 0, :], op=ALU.mult), reads=[r_gl, r_lnv], writes=[r_gl])
                    i = sc[0] % 2; sc[0] += 1
                    P.op('dve', R.tensor_tensor(out=stg[i][0:TW, :], in0=gl[0:TW, :], in1=lnv[0:TW, 1, :], op=ALU.add), reads=[r_gl, r_lnv], writes=[r_stg[i]])
                    P.op('act', R.activation(out=vln[0:TW, tt, :], in_=stg[i][0:TW, :], func=AF.Copy), reads=[r_stg[i]], writes=[r_vln])
                    if sample:
                        P.dma('act', R.dma_start(out=nch_s[l, :, :], in_=stg[i][0:TW, :]), reads=[r_stg[i]], writes=[r_out])
                proj_tm(OFF['v_a'], 512, epi_va)

                ug = P.sb([128, 512], F32, st); r_ug = P.res()

                def epi_ua(cc, tb, ps_, rps):
                    gelu_to(gtmp, ug[:, 0:TBW], ps_[:, 0:TBW], TBW, 128, st, [rps], [r_ug])
                    pa_, rpa = next_pa()
                    ntile = 1 if sample else 4
                    for q in range(ntile):
                        tt = tb * 4 + q
                        if sample:
                            P.op('pe', R.matmul(pa_[:, 0:TW], lhsT=vln[0:TW, 0, cc * 128:(cc + 1) * 128], rhs=wsTs[:, cc, :], start=True, stop=True),
                                 reads=[r_vln, r_prm], writes=[rpa])
                        else:
                            P.op('pe', R.matmul(pa_[:, q * 128:(q + 1) * 128], lhsT=vln[:, tt, cc * 128:(cc + 1) * 128], rhs=wsT[:, cc, :], start=True, stop=True),
                                 reads=[r_vln, r_prm], writes=[rpa])
                    if sample:
                        sg = sgA[:, 0:16]; r_sg = r_sgA
                        P.op('dve', R.tensor_tensor(out=sg, in0=pa_[:, 0:TW], in1=bss[:, cc, :], op=ALU.add), reads=[rpa, r_prm], writes=[r_sg])
                        P.op('dve', R.tensor_tensor(out=bo[0][:, cc, 0:TW], in0=sg, in1=ug[:, 0:TW], op=ALU.mult), reads=[r_sg, r_ug], writes=[r_bo[0]])
                    else:
                        sg = sgA; r_sg = r_sgA
                        P.op('dve', R.tensor_tensor(out=sg[:].rearrange("p (a b) -> p a b", b=128), in0=pa_[:].rearrange("p (a b) -> p a b", b=128),
                                                                      in1=bc(bsb, bsb[:, cc, :].offset, [[512, 128], [0, 4], [1, 128]]), op=ALU.add), reads=[rpa, r_prm], writes=[r_sg])
                        P.op('dve', R.tensor_tensor(out=bo[0][:, cc, tb * 512:(tb + 1) * 512], in0=sg[:], in1=ug[:], op=ALU.mult), reads=[r_sg, r_ug], writes=[r_bo[0]])
                proj_fm(OFF['u_a'], 4, epi_ua)

                def mk_epi_gate(i_bo, st_):
                    sl = P.sb([128, 512], BF16, st_); r_sl = P.res()

                    def epi(cc, tb, ps_, rps):
                        P.op('act', R.activation(out=sl[:, 0:TBW], in_=ps_[:, 0:TBW], func=AF.Silu), reads=[rps], writes=[r_sl])
                        P.op('dve', R.tensor_tensor(out=bo[i_bo][:, cc, tb * 512:tb * 512 + TBW], in0=bo[i_bo][:, cc, tb * 512:tb * 512 + TBW], in1=sl[:, 0:TBW], op=ALU.mult),
                             reads=[r_sl, r_bo[i_bo]], writes=[r_bo[i_bo]])
                    return epi
                proj_fm(OFF['g_a'], 4, mk_epi_gate(0, st))
                P.barrier()

            with ExitStack() as st:
              if 'B' in PH:
                HPW = 136 if sample else 30 + TP
                TA = NS if sample else TP
                hp = P.sb([128, 4, HPW], F32, st); r_hp = P.res()
                yv = P.sb([128, 4, TA], F32, st); r_yv = P.res()
                if sample:
                    sct = P.sb([128, 512], F32, st); r_sct = P.res()
                    for b4 in range(4):
                        P.dma('sp', R.dma_start(out=sct[0:30, :], in_=sconv[l, b4, :, :]), writes=[r_sct])
                        ps_, rps = next_pm()
                        for cc in range(4):
                            P.op('pe', R.transpose(out=ps_[:, cc * 32:cc * 32 + 30], in_=sct[0:30, cc * 128:(cc + 1) * 128], identity=cs[0:30, C_ID:C_ID + 30]),
                                 reads=[r_sct, r_cs], writes=[rps])
                        P.op('dve', R.tensor_copy(out=hp[:, :, b4 * 34:b4 * 34 + 30], in_=ps_[:, 0:128].rearrange("p (a b) -> p a b", b=32)[:, :, 0:30]),
                             reads=[rps], writes=[r_hp])
                        P.dma('act', R.dma_start(out=ncv_s[l, b4, 0:26, :], in_=sconv[l, b4, 4:30, :]), writes=[r_out])
                else:
                    if ps_i == 0:
                        P.op('dve', R.memset(hp[:, :, 0:30], 0.0), writes=[r_hp])
                    else:
                        P.op('dve', R.tensor_copy(out=hp[:, :, 0:30], in_=halo[:]), reads=[r_halo], writes=[r_hp])

                def hp_dst(cc, tb):
                    if sample:
                        return bc(hp, hp[:, cc, 30:31].offset, [[4 * HPW, 128], [34, 4], [1, 4]])
                    return hp[:, cc, 30 + tb * 512:30 + (tb + 1) * 512]

                def ps_src(ps_):
                    if sample:
                        return ps_[:, 0:16].rearrange("p (a b) -> p a b", b=4)
                    return ps_[:, 0:512]

                def epi_bb(cc, tb, ps_, rps):
                    P.op('act', R.activation(out=hp_dst(cc, tb), in_=ps_src(ps_), func=AF.Sigmoid), reads=[rps], writes=[r_hp])
                proj_fm(OFF['b_b'], 4, epi_bb)

                def epi_ab(cc, tb, ps_, rps):
                    P.op('dve', R.tensor_tensor(out=hp_dst(cc, tb), in0=hp_dst(cc, tb), in1=ps_src(ps_), op=ALU.mult), reads=[rps, r_hp], writes=[r_hp])
                proj_fm(OFF['a_b'], 4, epi_ab)

                def hp_win(cc, j):
                    if sample:
                        return bc(hp, hp[:, cc, j:j + 1].offset, [[4 * HPW, 128], [34, 4], [1, 4]])
                    return hp[:, cc, j:j + TP]

                def yv_ap(cc):
                    if sample:
                        return yv[:, cc, 0:16].rearrange("p (a b) -> p a b", b=4)
                    return yv[:, cc, :]
                for cc in range(4):
                    eng = 'dve'
                    P.op(eng, R.tensor_scalar(out=yv_ap(cc), in0=hp_win(cc, 0), scalar1=wdw[:, cc, 0:1], scalar2=prm[:, cc, 3:4], op0=ALU.mult, op1=ALU.add),
                         reads=[r_hp, r_prm], writes=[r_yv])
                    for j in range(1, 31):
                        P.op(eng, R.scalar_tensor_tensor(out=yv_ap(cc), in0=hp_win(cc, j), scalar=wdw[:, cc, j:j + 1], in1=yv_ap(cc), op0=ALU.mult, op1=ALU.add),
                             reads=[r_hp, r_prm, r_yv], writes=[r_yv])
                if sample:
                    hc = P.sb([128, 4, 16], F32, st); r_hc = P.res()
                    for cc in range(4):
                        P.op('dve', R.tensor_copy(out=hc[:, cc, :].rearrange("p (b t) -> p b t", t=4),
                                                  in_=bc(hp, hp[:, cc, 30:31].offset, [[4 * HPW, 128], [34, 4], [1, 4]])), reads=[r_hp], writes=[r_hc])
                    ps_, rps = next_pm()
                    for cc in range(4):
                        P.op('pe', R.transpose(out=ps_[0:16, cc * 128:(cc + 1) * 128], in_=hc[:, cc, :], identity=ident), reads=[r_hc, r_cs], writes=[rps])
                    hs = P.sb([16, 512], F32, st); r_hs = P.res()
                    P.op('dve', R.tensor_copy(out=hs[:], in_=ps_[0:16, :]), reads=[rps], writes=[r_hs])
                    for b4 in range(4):
                        P.dma('act', R.dma_start(out=ncv_s[l, b4, 26:30, :], in_=hs[b4 * 4:b4 * 4 + 4, :]), reads=[r_hs], writes=[r_out])
                else:
                    P.op('dve', R.tensor_copy(out=halo[:], in_=hp[:, :, TP:TP + 30]), reads=[r_hp], writes=[r_halo])
                    if ps_i == NPASS - 1:
                        ps_, rps = next_pm()
                        for cc in range(4):
                            P.op('pe', R.transpose(out=ps_[0:30, cc * 128:(cc + 1) * 128], in_=hp[:, cc, TP:TP + 30], identity=ident), reads=[r_hp, r_cs], writes=[rps])
                        hs = P.sb([30, 512], F32, st); r_hs = P.res()
                        P.op('dve', R.tensor_copy(out=hs[:], in_=ps_[0:30, :]), reads=[rps], writes=[r_hs])
                        P.dma('act', R.dma_start(out=ncv_p[l, :, :], in_=hs[:]), reads=[r_hs], writes=[r_out])
                ysq = [P.sb([128, 512], F32, st) for _ in range(2)]; r_ysq = [P.res() for _ in range(2)]
                zT = P.sb([128, 4, TA], BF16, st); r_zT = P.res()
                mean = P.sb([128, 512], F32, st); rstd = P.sb([128, 512], F32, st); r_ms = P.res()
                for tb in range(NTB):
                    p1, rp1 = next_pm(); p2, rp2 = next_pm()
                    for cc in range(4):
                        P.op('pe', R.matmul(p1[:, 0:TBW], lhsT=onef, rhs=yv[:, cc, tb * 512:tb * 512 + TBW], start=(cc == 0), stop=(cc == 3)), reads=[r_yv, r_cs], writes=[rp1])
                    for cc in range(4):
                        P.op('act', R.activation(out=ysq[cc % 2][:, 0:TBW], in_=yv[:, cc, tb * 512:tb * 512 + TBW], func=AF.Square), reads=[r_yv], writes=[r_ysq[cc % 2]])
                        P.op('pe', R.matmul(p2[:, 0:TBW], lhsT=onef, rhs=ysq[cc % 2][:, 0:TBW], start=(cc == 0), stop=(cc == 3)), reads=[r_ysq[cc % 2], r_cs], writes=[rp2])
                    P.op('act', R.activation(out=mean[:, 0:TBW], in_=p1[:, 0:TBW], func=AF.Copy, scale=1.0 / 512), reads=[rp1], writes=[r_ms])
                    P.op('dve', R.tensor_tensor(out=rstd[:, 0:TBW], in0=mean[:, 0:TBW], in1=mean[:, 0:TBW], op=ALU.mult), reads=[r_ms], writes=[r_ms])
                    P.op('dve', R.scalar_tensor_tensor(out=rstd[:, 0:TBW], in0=p2[:, 0:TBW], scalar=1.0 / 512, in1=rstd[:, 0:TBW], op0=ALU.mult, op1=ALU.subtract), reads=[rp2, r_ms], writes=[r_ms])
                    P.op('dve', R.tensor_scalar(out=rstd[:, 0:TBW], in0=rstd[:, 0:TBW], scalar1=EPS, scalar2=None, op0=ALU.add), reads=[r_ms], writes=[r_ms])
                    P.op('act', R.activation(out=rstd[:, 0:TBW], in_=rstd[:, 0:TBW], func=AF.Sqrt), reads=[r_ms], writes=[r_ms])
                    P.op('dve', R.reciprocal(out=rstd[:, 0:TBW], in_=rstd[:, 0:TBW]), reads=[r_ms], writes=[r_ms])
                    for cc in range(4):
                        ysl = yv[:, cc, tb * 512:tb * 512 + TBW]
                        P.op('dve', R.tensor_tensor(out=ysl, in0=ysl, in1=mean[:, 0:TBW], op=ALU.subtract), reads=[r_yv, r_ms], writes=[r_yv])
                        P.op('dve', R.tensor_tensor(out=ysl, in0=ysl, in1=rstd[:, 0:TBW], op=ALU.mult), reads=[r_yv, r_ms], writes=[r_yv])
                        P.op('act', R.activation(out=zT[:, cc, tb * 512:tb * 512 + TBW], in_=ysl, func=AF.Silu, scale=prm[:, cc, 0:1], bias=prm[:, cc, 1:2]),
                             reads=[r_yv, r_prm], writes=[r_zT])
                sgb = P.sb([128, 4, TA], BF16, st); r_sgb = P.res()

                def epi_gb(cc, tb, ps_, rps):
                    P.op('act', R.activation(out=sgb[:, cc, tb * 512:tb * 512 + TBW], in_=ps_[:, 0:TBW], func=AF.Silu), reads=[rps], writes=[r_sgb])
                proj_fm(OFF['g_b'], 4, epi_gb)

                def epi_pw(cc, tb, ps_, rps):
                    P.op('dve', R.scalar_tensor_tensor(out=bo[1][:, cc, tb * 512:tb * 512 + TBW], in0=ps_[:, 0:TBW], scalar=prm[:, cc, 2:3], in1=sgb[:, cc, tb * 512:tb * 512 + TBW], op0=ALU.add, op1=ALU.mult),
                         reads=[rps, r_prm, r_sgb], writes=[r_bo[1]])
                proj_fm(0, 4, epi_pw, nk=4, rhsT=zT, r_rhs=r_zT, Wt=wpw, rW=r_prm)
                P.barrier()

            with ExitStack() as st:
              if 'C' in PH:
                qT = P.sb([128, 4, TP], BF16, st); r_qT = P.res()
                lf = P.sb([128, 8, 4], F32, st); r_lf = P.res()
                tmp4 = P.sb([128, 8, 4], F32, st); r_t4 = P.res()
                offs = P.sb([128, 9, 4], F32, st); r_offs = P.res()
                tot = P.sb([128, 8, 4], F32, st)
                bias = P.sb([128, 2, 32, 4], F32, st); r_bias = P.res()
                st1 = ExitStack()
                st.enter_context(st1)
                kTl = P.sb([128, 4, TP], BF16, st1); r_kTl = P.res()
                stg = [P.sb([128, 512], F32, st1) for _ in range(2)]; r_stg = [P.res() for _ in range(2)]
                vb = P.sb([128, 8, 512], BF16, st1); r_vb = P.res()
                sc = [0]

                def epi_q(cc, tb, ps_, rps):
                    P.op('act', R.activation(out=qT[:, cc, tb * 512:tb * 512 + TBW], in_=ps_[:, 0:TBW], func=AF.Copy), reads=[rps], writes=[r_qT])
                proj_fm(OFF['q_c'], 4, epi_q)

                def epi_k(cc, tb, ps_, rps):
                    P.op('dve', R.tensor_copy(out=kTl[:, cc, tb * 512:tb * 512 + TBW], in_=ps_[:, 0:TBW]), reads=[rps], writes=[r_kTl])
                proj_fm(OFF['k_c'], 4, epi_k)
                if not sample:
                    for cc in range(4):
                        P.dma('act', R.dma_start(out=kT_h[cc * 128:(cc + 1) * 128, t_base:t_base + TP], in_=kTl[:, cc, :]), reads=[r_kTl], writes=[r_kTh])

                def mk_epi_kv(dst_p, dst_s, is_v):
                    def epi(tt, ps_, rps):
                        i = sc[0] % 2; sc[0] += 1
                        P.op('act', R.activation(out=stg[i][0:TW, :], in_=ps_[0:TW, :], func=AF.Copy), reads=[rps], writes=[r_stg[i]])
                        if is_v:
                            P.op('dve', R.tensor_copy(out=vb[0:TW, tt, :], in_=ps_[0:TW, :]), reads=[rps], writes=[r_vb])
                        if sample:
                            P.dma('act', R.dma_start(out=dst_s[l, :, :], in_=stg[i][0:TW, :]), reads=[r_stg[i]], writes=[r_out])
                        else:
                            P.dma('act', R.dma_start(out=dst_p[l, t_base + tt * 128:t_base + (tt + 1) * 128, :], in_=stg[i][:, :]), reads=[r_stg[i]], writes=[r_out])
                    return epi
                proj_tm(OFF['k_c'], 512, mk_epi_kv(nk_p, nk_s, False))
                proj_tm(OFF['v_c'], 512, mk_epi_kv(nv_p, nv_s, True))
                if not sample:
                    P.dma('act', R.dma_start(out=v_h[t_base:t_base + TP, :].rearrange("(a p) c -> p a c", p=128), in_=vb[:]), reads=[r_vb], writes=[r_vh])


                def epi_f(tt, ps_, rps):
                    P.op('dve', R.tensor_tensor(out=lf[0:TW, tt, :], in0=ps_[0:TW, 0:4], in1=bfb[0:TW, :], op=ALU.add), reads=[rps, r_prm], writes=[r_lf])
                proj_tm(OFF['f_c'], 4, epi_f)
                lfv = lf[0:TW, 0:NTT, :]; t4v = tmp4[0:TW, 0:NTT, :]
                P.op('act', R.activation(out=t4v, in_=lfv, func=AF.Abs), reads=[r_lf], writes=[r_t4])
                P.op('act', R.activation(out=t4v, in_=t4v, func=AF.Exp, scale=-1.0), reads=[r_t4], writes=[r_t4])
                P.op('act', R.activation(out=t4v, in_=t4v, func=AF.Ln, bias=1.0), reads=[r_t4], writes=[r_t4])
                P.op('dve', R.tensor_scalar(out=lfv, in0=lfv, scalar1=0.0, scalar2=None, op0=ALU.min), reads=[r_lf], writes=[r_lf])
                P.op('dve', R.tensor_tensor(out=lfv, in0=lfv, in1=t4v, op=ALU.subtract), reads=[r_lf, r_t4], writes=[r_lf])
                with nc.allow_non_contiguous_dma(reason="logf rows are 16B"):
                    if sample:
                        P.dma('act', R.dma_start(out=nlf_s[l, :, :], in_=lf[0:TW, 0, :]), reads=[r_lf], writes=[r_out])
                    else:
                        P.dma('act', R.dma_start(out=nlf_p[l, t_base:t_base + TP, :].rearrange("(a p) h -> p a h", p=128), in_=lf[:]), reads=[r_lf], writes=[r_out])

                if not sample:
                    pa_, rpa = next_pa()
                    P.op('pe', R.matmul(pa_[:, 0:32], lhsT=tri, rhs=lf[:].rearrange("p a h -> p (a h)"), start=True, stop=True), reads=[r_lf, r_cs], writes=[rpa])
                    P.op('pe', R.matmul(pa_[:, 32:64], lhsT=onef, rhs=lf[:].rearrange("p a h -> p (a h)"), start=True, stop=True), reads=[r_lf, r_cs], writes=[rpa])
                    P.op('dve', R.tensor_copy(out=tot[:].rearrange("p a h -> p (a h)"), in_=pa_[:, 32:64]), reads=[rpa], writes=[r_offs])
                    if ps_i == 0:
                        P.op('dve', R.memset(offs[:, 0, :], 0.0), writes=[r_offs])
                    else:
                        P.op('dve', R.tensor_copy(out=offs[:, 0, :], in_=carry[:]), reads=[r_carry], writes=[r_offs])
                    for tt in range(8):
                        P.op('dve', R.tensor_tensor(out=offs[:, tt + 1, :], in0=offs[:, tt, :], in1=tot[:, tt, :], op=ALU.add), reads=[r_offs], writes=[r_offs])
                    P.op('dve', R.tensor_copy(out=carry[:], in_=offs[:, 8, :]), reads=[r_offs], writes=[r_carry])
                    P.op('dve', R.tensor_tensor(out=ckh[:, ps_i * 8:(ps_i + 1) * 8, :], in0=pa_[:, 0:32].rearrange("p (a h) -> p a h", h=4), in1=offs[:, 0:8, :], op=ALU.add),
                         reads=[rpa, r_offs], writes=[r_ckh])
                    nkb_all = (ps_i + 1) * 8
                    for sb in range(2):
                        P.op('dve', R.tensor_tensor(out=bias[:, sb, 0:nkb_all, :], in0=bc(offs, offs[:, sb * 4, :].offset, [[36, 128], [0, nkb_all], [1, 4]]),
                                                                     in1=ckh[:, 0:nkb_all, :], op=ALU.subtract), reads=[r_offs, r_ckh], writes=[r_bias])
                    P.barrier()
                    st1.close()
                    kTs = [P.sb([128, SEQ], BF16, st) for _ in range(2)]; r_kTs = [P.res() for _ in range(2)]
                    vs = [P.sb([128, 32, 128], BF16, st) for _ in range(2)]; r_vs = [P.res() for _ in range(2)]
                    pt_ = [P.sb([128, 512], BF16, st) for _ in range(3)]; r_pt = [P.res() for _ in range(3)]
                    rs = P.sb([128, 512], F32, st); r_rs = P.res()
                    pc = [0]
                    for h in range(4):
                        a = h % 2
                        nk_tok = nkb_all * 128
                        P.dma('sp', R.dma_start(out=kTs[a][:, 0:nk_tok], in_=kT_h[h * 128:(h + 1) * 128, 0:nk_tok]), reads=[r_kTh], writes=[r_kTs[a]])
                        with nc.allow_non_contiguous_dma(reason="V head slices 256B"):
                            P.dma('sp', R.dma_start(out=vs[a][:, 0:nkb_all, :], in_=v_h[0:nk_tok, h * 128:(h + 1) * 128].rearrange("(a p) c -> p a c", p=128)), reads=[r_vh], writes=[r_vs[a]])
                        for sb in range(2):
                            nkb = ps_i * 8 + sb * 4 + 4
                            for kb in range(nkb):
                                pa_, rpa = next_pa()
                                P.op('pe', R.matmul(pa_[:], lhsT=kTs[a][:, kb * 128:(kb + 1) * 128], rhs=qT[:, h, sb * 512:(sb + 1) * 512], start=True, stop=True),
                                     reads=[r_kTs[a], r_qT], writes=[rpa])
                                i = pc[0] % 3; pc[0] += 1
                                P.op('act', R.activation(out=pt_[i][:], in_=pa_[:], func=AF.Exp, bias=bias[:, sb, kb, h:h + 1], scale=SCALE),
                                     reads=[rpa, r_bias], writes=[r_pt[i]])
                                d = kb - (ps_i * 8 + sb * 4)
                                if d >= 0:
                                    P.op('pool', R.tensor_tensor(out=pt_[i][:], in0=pt_[i][:], in1=maskb[:, d, :], op=ALU.mult), reads=[r_pt[i], r_cb], writes=[r_pt[i]])
                                P.op('pe', R.matmul(po[:], lhsT=vs[a][:, kb, :], rhs=pt_[i][:], start=(kb == 0), stop=(kb == nkb - 1)), reads=[r_vs[a], r_pt[i]], writes=[r_po])
                                P.op('pe', R.matmul(pq[:], lhsT=oneb[:], rhs=pt_[i][:], start=(kb == 0), stop=(kb == nkb - 1)), reads=[r_cb, r_pt[i]], writes=[r_pq])
                            P.op('dve', R.reciprocal(out=rs[:], in_=pq[:]), reads=[r_pq], writes=[r_rs])
                            P.op('dve', R.tensor_tensor(out=bo[2][:, h, sb * 512:(sb + 1) * 512], in0=po[:], in1=rs[:], op=ALU.mult), reads=[r_po, r_rs], writes=[r_bo[2]])
                else:
                    ptb = P.sb([128, 4 * NPG], I32, st); r_ptb = P.res()
                    idx = P.sb([128, 4 * NPG], I32, st)
                    ptT = P.sb([NPG, 4], I32, st); idl = P.sb([NPG, 4], I32, st)
                    with nc.allow_non_contiguous_dma(reason="page table broadcast / transpose"):
                        P.dma('sp', R.dma_start(out=ptb[:], in_=pt.rearrange("b g -> (b g)").partition_broadcast(128)), writes=[r_ptb])
                        P.dma('sp', R.dma_start(out=ptT[:], in_=pt.rearrange("b g -> g b")), writes=[r_ptb])
                    P.op('dve', R.tensor_scalar(out=idx[:], in0=ptb[:], scalar1=128.0, scalar2=float(l * NPOOL * 128), op0=ALU.mult, op1=ALU.add), reads=[r_ptb], writes=[r_ptb])
                    P.op('dve', R.tensor_scalar(out=idx[:], in0=idx[:], scalar1=cs[:, C_IOTA:C_IOTA + 1], scalar2=None, op0=ALU.add), reads=[r_ptb, r_cs], writes=[r_ptb])
                    P.op('dve', R.tensor_scalar(out=idl[:], in0=ptT[:], scalar1=float(l * NPOOL), scalar2=None, op0=ALU.add), reads=[r_ptb], writes=[r_ptb])
                    kpg = [P.sb([128, 512], F32, st) for _ in range(2)]; r_kpg = [P.res() for _ in range(2)]
                    vpg = [P.sb([128, 512], BF16, st) for _ in range(2)]; r_vpg = [P.res() for _ in range(2)]
                    kTp = [P.sb([128, 4, 128], BF16, st) for _ in range(2)]; r_kTp = [P.res() for _ in range(2)]
                    lfp = P.sb([NPG, 128, 4], F32, st); r_lfp = P.res()
                    cp = P.sb([NPG, 128, 4], F32, st); r_cp = P.res()
                    one64 = P.sb([NPG, 128], F32, st)
                    P.op('dve', R.memset(one64[:], 1.0), writes=[r_cp])
                    sfx = P.sb([NPG, 4], F32, st)
                    biasS = P.sb([128, 4, NPG], F32, st); r_bS = P.res()
                    GS = min(32, NPG)
                    sx = P.sb([128, 512], F32, st); r_sx = P.res()
                    pts = P.sb([128, 32, 16], BF16, st); r_pts = P.res()
                    oacc = P.sb([128, 4, 16], F32, st); sacc = P.sb([128, 4, 16], F32, st); r_acc = P.res()
                    red = P.sb([128, 2, 16], F32, st); r_red = P.res()
                    pa_, rpa = next_pa()
                    for h in range(4):
                        P.op('pe', R.matmul(pa_[0:16, h * 16:(h + 1) * 16], lhsT=kTl[:, h, 0:16], rhs=qT[:, h, 0:16], start=True, stop=True), reads=[r_kTl, r_qT], writes=[rpa])
                    P.op('pe', R.matmul(pa_[0:16, 64:68], lhsT=cs[0:16, C_BLKM:C_BLKM + 16], rhs=lf[0:16, 0, :], start=True, stop=True), reads=[r_lf, r_cs], writes=[rpa])
                    cn = P.sb([16, 4], F32, st); r_cn = P.res()
                    P.op('dve', R.tensor_scalar(out=cn[:], in0=pa_[0:16, 64:68], scalar1=-1.0, scalar2=None, op0=ALU.mult), reads=[rpa], writes=[r_cn])
                    pn = P.sb([16, 4, 16], BF16, st); r_pn = P.res()
                    for h in range(4):
                        P.op('act', R.activation(out=pn[:, h, :], in_=pa_[0:16, h * 16:(h + 1) * 16], func=AF.Exp, bias=cn[:, h:h + 1], scale=SCALE), reads=[rpa, r_cn], writes=[r_pn])
                    P.op('dve', R.tensor_tensor(out=pn[:], in0=pn[:], in1=bc(blkmb, 0, [[16, 16], [0, 4], [1, 16]]), op=ALU.mult), reads=[r_pn, r_cb], writes=[r_pn])
                    pa2, rpa2 = next_pa()
                    for h in range(4):
                        P.op('pe', R.matmul(pa2[:, h * 16:(h + 1) * 16], lhsT=vb[0:16, 0, h * 128:(h + 1) * 128], rhs=pn[:, h, :], start=True, stop=True), reads=[r_vb, r_pn], writes=[rpa2])
                    P.op('pe', R.matmul(pa2[:, 64:128], lhsT=oneb[0:16, :], rhs=pn[:].rearrange("p h q -> p (h q)"), start=True, stop=True), reads=[r_cb, r_pn], writes=[rpa2])
                    P.op('dve', R.tensor_copy(out=oacc[:].rearrange("p h q -> p (h q)"), in_=pa2[:, 0:64]), reads=[rpa2], writes=[r_acc])
                    P.op('dve', R.tensor_copy(out=sacc[:].rearrange("p h q -> p (h q)"), in_=pa2[:, 64:128]), reads=[rpa2], writes=[r_acc])
                    gc = [0]
                    for b4 in range(4):
                        P.dma('pool', R.indirect_dma_start(out=lfp[:].rearrange("p k h -> p (k h)"), out_offset=None, in_=cache_lf,
                                                                            in_offset=bass.IndirectOffsetOnAxis(ap=idl[:, b4:b4 + 1], axis=0)), reads=[r_ptb], writes=[r_lfp])
                        for h in range(4):
                            P.op('dve', R.tensor_tensor_scan(out=cp[:, :, h], data0=one64[:], data1=lfp[:, :, h], initial=0.0, op0=ALU.mult, op1=ALU.add), reads=[r_lfp, r_cp], writes=[r_cp])
                        pa_, rpa = next_pa()
                        P.op('pe', R.matmul(pa_[0:NPG, 0:4], lhsT=cs[0:NPG, C_UT:C_UT + NPG], rhs=cp[:, 127, :], start=True, stop=True), reads=[r_cp, r_cs], writes=[rpa])
                        P.op('dve', R.tensor_tensor(out=sfx[:], in0=pa_[0:NPG, 0:4], in1=cp[:, 127, :], op=ALU.add), reads=[rpa, r_cp], writes=[r_cp])
                        P.op('dve', R.tensor_tensor(out=cp[:], in0=bc(sfx, 0, [[4, NPG], [0, 128], [1, 4]]), in1=cp[:], op=ALU.subtract), reads=[r_cp], writes=[r_cp])
                        pa_, rpa = next_pa()
                        for h in range(4):
                            P.op('pe', R.transpose(out=pa_[:, h * NPG:(h + 1) * NPG], in_=cp[:, :, h], identity=cs[0:NPG, C_ID:C_ID + NPG]), reads=[r_cp, r_cs], writes=[rpa])
                        P.op('dve', R.tensor_copy(out=biasS[:].rearrange("p h g -> p (h g)"), in_=pa_[:, 0:4 * NPG]), reads=[rpa], writes=[r_bS])
                        for g0 in range(0, NPG, GS):
                            pS, rpS = next_pa()
                            for g in range(g0, g0 + GS):
                                a = gc[0] % 2; gc[0] += 1
                                col = b4 * NPG + g
                                P.dma('pool', R.indirect_dma_start(out=kpg[a][:], out_offset=None, in_=cache_k,
                                                                                         in_offset=bass.IndirectOffsetOnAxis(ap=idx[:, col:col + 1], axis=0)), reads=[r_ptb], writes=[r_kpg[a]])
                                P.dma('pool', R.indirect_dma_start(out=vpg[a][:], out_offset=None, in_=cache_v,
                                                                                         in_offset=bass.IndirectOffsetOnAxis(ap=idx[:, col:col + 1], axis=0)), reads=[r_ptb, r_pts], writes=[r_vpg[a]])
                                ps_, rps = next_pm()
                                for h in range(4):
                                    P.op('pe', R.transpose(out=ps_[:, h * 128:(h + 1) * 128], in_=kpg[a][:, h * 128:(h + 1) * 128], identity=ident), reads=[r_kpg[a], r_cs], writes=[rps])
                                P.op(('act' if g % 2 else 'dve'), (R.activation(out=kTp[a][:].rearrange("p h k -> p (h k)"), in_=ps_[:], func=AF.Copy)) if g % 2 else
                                     (R.tensor_copy(out=kTp[a][:].rearrange("p h k -> p (h k)"), in_=ps_[:])), reads=[rps], writes=[r_kTp[a]])
                                for h in range(4):
                                    o = (g - g0) * 16 + h * 4
                                    P.op('pe', R.matmul(pS[:, o:o + 4], lhsT=kTp[a][:, h, :], rhs=qT[:, h, b4 * 4:b4 * 4 + 4], start=True, stop=True), reads=[r_kTp[a], r_qT], writes=[rpS])
                            for h in range(4):
                                P.op('dve', R.scalar_tensor_tensor(out=sx[:, 0:GS * 16].rearrange("p (g h q) -> p g h q", h=4, q=4)[:, :, h, :], in0=pS[:, 0:GS * 16].rearrange("p (g h q) -> p g h q", h=4, q=4)[:, :, h, :], scalar=SCALE,
                                                                   in1=bc(biasS, biasS[:, h, g0:g0 + 1].offset, [[4 * NPG, 128], [1, GS], [0, 4]]), op0=ALU.mult, op1=ALU.add), reads=[rpS, r_bS], writes=[r_sx])
                            P.op('act', R.activation(out=pts[:, 0:GS, :].rearrange("p g c -> p (g c)"), in_=sx[:, 0:GS * 16], func=AF.Exp), reads=[r_sx], writes=[r_pts])
                            pov, rpov = next_pm()
                            psv, rpsv = next_pm()
                            for g in range(g0, g0 + GS):
                                a = gc[0] % 2; gc[0] += 1
                                col = b4 * NPG + g
                                P.dma('pool', R.indirect_dma_start(out=vpg[a][:], out_offset=None, in_=cache_v,
                                                                   in_offset=bass.IndirectOffsetOnAxis(ap=idx[:, col:col + 1], axis=0)), reads=[r_ptb], writes=[r_vpg[a]])
                                for h in range(4):
                                    o = (g - g0) * 16 + h * 4
                                    P.op('pe', R.matmul(pov[:, o:o + 4], lhsT=vpg[a][:, h * 128:(h + 1) * 128], rhs=pts[:, g - g0, h * 4:h * 4 + 4], start=True, stop=True),
                                         reads=[r_vpg[a], r_pts], writes=[rpov])
                                o = (g - g0) * 16
                                P.op('pe', R.matmul(psv[:, o:o + 16], lhsT=oneb[:], rhs=pts[:, g - g0, :], start=True, stop=True), reads=[r_cb, r_pts], writes=[rpsv])
                            P.op('dve', R.tensor_reduce(out=red[:, 0, :], in_=pov[:, 0:GS * 16].rearrange("p (g c) -> p c g", c=16), axis=mybir.AxisListType.X, op=ALU.add), reads=[rpov], writes=[r_red])
                            P.op('dve', R.tensor_reduce(out=red[:, 1, :], in_=psv[:, 0:GS * 16].rearrange("p (g c) -> p c g", c=16), axis=mybir.AxisListType.X, op=ALU.add), reads=[rpsv], writes=[r_red])
                            P.op('dve', R.tensor_tensor(out=oacc[:, :, b4 * 4:b4 * 4 + 4], in0=oacc[:, :, b4 * 4:b4 * 4 + 4], in1=red[:, 0, :].rearrange("p (h q) -> p h q", q=4), op=ALU.add), reads=[r_red, r_acc], writes=[r_acc])
                            P.op('dve', R.tensor_tensor(out=sacc[:, :, b4 * 4:b4 * 4 + 4], in0=sacc[:, :, b4 * 4:b4 * 4 + 4], in1=red[:, 1, :].rearrange("p (h q) -> p h q", q=4), op=ALU.add), reads=[r_red, r_acc], writes=[r_acc])
                    P.op('dve', R.reciprocal(out=sacc[:], in_=sacc[:]), reads=[r_acc], writes=[r_acc])
                    P.op('dve', R.tensor_tensor(out=bo[2][:, :, 0:16], in0=oacc[:], in1=sacc[:], op=ALU.mult), reads=[r_acc], writes=[r_bo[2]])
                    P.barrier()
                proj_fm(OFF['g_c'], 4, mk_epi_gate(2, st))
                P.barrier()

            with ExitStack() as st:
              if 'M' in PH:
                qT = P.sb([128, 4, TP], BF16, st); r_qT = P.res()

                def epi_q(cc, tb, ps_, rps):
                    P.op('act', R.activation(out=qT[:, cc, tb * 512:tb * 512 + TBW], in_=ps_[:, 0:TBW], func=AF.Copy), reads=[rps], writes=[r_qT])
                proj_fm(OFF['q_m'], 4, epi_q)
                pt_ = [P.sb([128, 512], BF16, st) for _ in range(3)]; r_pt = [P.res() for _ in range(3)]
                rs = P.sb([128, 512], F32, st); r_rs = P.res()
                pc = [0]
                if not sample:
                    for h in range(4):
                        for sb in range(2):
                            for mb in range(2):
                                pa_, rpa = next_pa()
                                P.op('pe', R.matmul(pa_[:], lhsT=kTm[:, h, mb * 128:(mb + 1) * 128], rhs=qT[:, h, sb * 512:(sb + 1) * 512], start=True, stop=True), reads=[r_mem, r_qT], writes=[rpa])
                                i = pc[0] % 3; pc[0] += 1
                                P.op('act', R.activation(out=pt_[i][:], in_=pa_[:], func=AF.Exp, scale=SCALE), reads=[rpa], writes=[r_pt[i]])
                                P.op('pe', R.matmul(po[:], lhsT=vm[:, mb, h * 128:(h + 1) * 128], rhs=pt_[i][:], start=(mb == 0), stop=(mb == 1)), reads=[r_mem, r_pt[i]], writes=[r_po])
                                P.op('pe', R.matmul(pq[:], lhsT=oneb[:], rhs=pt_[i][:], start=(mb == 0), stop=(mb == 1)), reads=[r_cb, r_pt[i]], writes=[r_pq])
                            P.op('dve', R.reciprocal(out=rs[:], in_=pq[:]), reads=[r_pq], writes=[r_rs])
                            P.op('dve', R.tensor_tensor(out=bo[3][:, h, sb * 512:(sb + 1) * 512], in0=po[:], in1=rs[:], op=ALU.mult), reads=[r_po, r_rs], writes=[r_bo[3]])
                else:
                    mk = P.sb([128, 2, 512], F32, st); r_mk = P.res()
                    mv = P.sb([128, 2, 512], BF16, st); r_mv = P.res()
                    kTs = P.sb([128, 4, 256], BF16, st); r_kTs = P.res()
                    for b4 in range(4):
                        P.dma('sp', R.dma_start(out=mk[:], in_=cmk[l, b4].rearrange("(a p) c -> p a c", p=128)), writes=[r_mk])
                        P.dma('pool', R.dma_start(out=mv[:], in_=cmv[l, b4].rearrange("(a p) c -> p a c", p=128)), writes=[r_mv])
                        for mb in range(2):
                            ps_, rps = next_pm()
                            for h in range(4):
                                P.op('pe', R.transpose(out=ps_[:, h * 128:(h + 1) * 128], in_=mk[:, mb, h * 128:(h + 1) * 128], identity=ident), reads=[r_mk, r_cs], writes=[rps])
                            P.op('dve', R.tensor_copy(out=kTs[:, :, mb * 128:(mb + 1) * 128], in_=ps_[:].rearrange("p (h k) -> p h k", k=128)), reads=[rps], writes=[r_kTs])
                        pa_, rpa = next_pa()
                        for mb in range(2):
                            for h in range(4):
                                o = mb * 16 + h * 4
                                P.op('pe', R.matmul(pa_[:, o:o + 4], lhsT=kTs[:, h, mb * 128:(mb + 1) * 128], rhs=qT[:, h, b4 * 4:b4 * 4 + 4], start=True, stop=True), reads=[r_kTs, r_qT], writes=[rpa])
                        P.op('act', R.activation(out=pt_[0][:, 0:32], in_=pa_[:, 0:32], func=AF.Exp, scale=SCALE), reads=[rpa], writes=[r_pt[0]])
                        pov, rpov = next_pm()
                        for h in range(4):
                            for mb in range(2):
                                o = mb * 16 + h * 4
                                P.op('pe', R.matmul(pov[:, h * 4:h * 4 + 4], lhsT=mv[:, mb, h * 128:(h + 1) * 128], rhs=pt_[0][:, o:o + 4], start=(mb == 0), stop=(mb == 1)), reads=[r_mv, r_pt[0]], writes=[rpov])
                        for mb in range(2):
                            P.op('pe', R.matmul(pov[:, 16:32], lhsT=oneb[:], rhs=pt_[0][:, mb * 16:(mb + 1) * 16], start=(mb == 0), stop=(mb == 1)), reads=[r_cb, r_pt[0]], writes=[rpov])
                        P.op('dve', R.reciprocal(out=rs[:, 0:16], in_=pov[:, 16:32]), reads=[rpov], writes=[r_rs])
                        P.op('dve', R.tensor_tensor(out=bo[3][:, :, b4 * 4:b4 * 4 + 4], in0=pov[:, 0:16].rearrange("p (h q) -> p h q", q=4), in1=rs[:, 0:16].rearrange("p (h q) -> p h q", q=4), op=ALU.mult),
                             reads=[rpov, r_rs], writes=[r_bo[3]])
                proj_fm(OFF['g_m'], 4, mk_epi_gate(3, st))
                P.barrier()

            with ExitStack() as st:
              if 'merge' in PH:
                acc = P.sb([128, 8, 512], F32, st); r_acc = P.res()
                wb = [P.sb([128, 4, 512], BF16, st) for _ in range(2)]; r_wb = [P.res() for _ in range(2)]
                sg = [P.sb([128, 512], F32, st) for _ in range(2)]; r_sg = [P.res() for _ in range(2)]
                wbc = [0]; sgc = [0]
                for mg in range(4):
                    for i in range(4):
                        a = wbc[0] % 2; wbc[0] += 1
                        P.dma('pool', R.dma_start(out=wb[a][:], in_=w_br[l, i, :, mg * 512:(mg + 1) * 512].rearrange("(k p) c -> p k c", p=128)), writes=[r_wb[a]])

                        def epi(cc, tb, ps_, rps, i=i, a=a, mg=mg):
                            pp, rpp = next_pa()
                            for kc in range(4):
                                P.op('pe', R.matmul(pp[:, 0:TBW], lhsT=wb[a][:, kc, cc * 128:(cc + 1) * 128], rhs=bo[i][:, kc, tb * 512:tb * 512 + TBW], start=(kc == 0), stop=(kc == 3)), reads=[r_wb[a], r_bo[i]], writes=[rpp])
                            s = sgc[0] % 2; sgc[0] += 1
                            P.op('act', R.activation(out=sg[s][:, 0:TBW], in_=ps_[:, 0:TBW], func=AF.Sigmoid), reads=[rps], writes=[r_sg[s]])
                            dst = acc[:, cc * 2 + tb, 0:TBW]
                            if i == 0:
                                P.op('dve', R.tensor_tensor(out=dst, in0=sg[s][:, 0:TBW], in1=pp[:, 0:TBW], op=ALU.mult), reads=[r_sg[s], rpp], writes=[r_acc])
                            else:
                                P.op('dve', R.tensor_tensor(out=sg[s][:, 0:TBW], in0=sg[s][:, 0:TBW], in1=pp[:, 0:TBW], op=ALU.mult), reads=[r_sg[s], rpp], writes=[r_sg[s]])
                                P.op('dve', R.tensor_tensor(out=dst, in0=dst, in1=sg[s][:, 0:TBW], op=ALU.add), reads=[r_sg[s], r_acc], writes=[r_acc])
                            if i == 3:
                                P.op('act', R.activation(out=hT[:, mg * 4 + cc, tb * 512:tb * 512 + TBW], in_=dst, func=AF.Copy), reads=[r_acc], writes=[r_hT])
                        proj_fm(OFF['gate'] + i * D + mg * 512, 4, epi)
                P.barrier()

            with ExitStack() as st:
              if 'out' in PH:
                stg = [P.sb([128, 512], F32, st) for _ in range(2)]; r_stg = [P.res() for _ in range(2)]
                xr = [P.sb([128, 512], F32, st) for _ in range(2)]; r_xr = [P.res() for _ in range(2)]
                sc = [0]
                for cg in range(4):
                    def epi(tt, ps_, rps, cg=cg):
                        i = sc[0] % 2; sc[0] += 1
                        P.dma('sp', R.dma_start(out=xr[i][0:TW, :], in_=x_in[t_base + tt * TW:t_base + (tt + 1) * TW, cg * 512:(cg + 1) * 512]), reads=[r_xin], writes=[r_xr[i]])
                        P.op('dve', R.scalar_tensor_tensor(out=stg[i][0:TW, :], in0=xr[i][0:TW, :], scalar=ALPHA, in1=ps_[0:TW, :], op0=ALU.mult, op1=ALU.add), reads=[rps, r_xr[i]], writes=[r_stg[i]])
                        P.dma('act', R.dma_start(out=ypre[tt * TW:(tt + 1) * TW, cg * 512:(cg + 1) * 512], in_=stg[i][0:TW, :]), reads=[r_stg[i]], writes=[r_ypre])
                    proj_tm(0, 512, epi, wsrc=w_out[l][:, cg * 512:(cg + 1) * 512], lhs=hT, r_lhs=r_hT)
                P.barrier()
            with ExitStack() as st:
              if 'ln' in PH:
                yt = [P.sb([128, D], F32, st) for _ in range(2)]; r_yt = [P.res() for _ in range(2)]
                stt = P.sb([128, 32], F32, st); r_stt = P.res()
                lnf = P.sb([128, 2, D], F32, st); r_lnf = P.res()
                with nc.allow_non_contiguous_dma(reason="param broadcast"):
                    P.dma('sp', R.dma_start(out=lnf[:, 0, :], in_=ln_g[l].partition_broadcast(128)), writes=[r_lnf])
                    P.dma('sp', R.dma_start(out=lnf[:, 1, :], in_=ln_b[l].partition_broadcast(128)), writes=[r_lnf])
                for tt in range(NTT):
                    i = tt % 2
                    P.dma('sp', R.dma_start(out=yt[i][0:TW, :], in_=ypre[tt * TW:(tt + 1) * TW, :]), reads=[r_ypre], writes=[r_yt[i]])
                    for q in range(4):
                        P.op('dve', R.bn_stats(out=stt[0:TW, q * 6:(q + 1) * 6], in_=yt[i][0:TW, q * 512:(q + 1) * 512]), reads=[r_yt[i]], writes=[r_stt])
                    P.op('dve', R.bn_aggr(out=stt[0:TW, 24:26], in_=stt[0:TW, 0:24]), reads=[r_stt], writes=[r_stt])
                    P.op('dve', R.tensor_scalar(out=stt[0:TW, 25:26], in0=stt[0:TW, 25:26], scalar1=EPS, scalar2=None, op0=ALU.add), reads=[r_stt], writes=[r_stt])
                    P.op('act', R.activation(out=stt[0:TW, 25:26], in_=stt[0:TW, 25:26], func=AF.Sqrt), reads=[r_stt], writes=[r_stt])
                    P.op('dve', R.reciprocal(out=stt[0:TW, 25:26], in_=stt[0:TW, 25:26]), reads=[r_stt], writes=[r_stt])
                    P.op('dve', R.tensor_scalar(out=yt[i][0:TW, :], in0=yt[i][0:TW, :], scalar1=stt[0:TW, 24:25], scalar2=stt[0:TW, 25:26], op0=ALU.subtract, op1=ALU.mult), reads=[r_yt[i], r_stt], writes=[r_yt[i]])
                    P.op('pool', R.tensor_tensor(out=yt[i][0:TW, :], in0=yt[i][0:TW, :], in1=lnf[0:TW, 0, :], op=ALU.mult), reads=[r_yt[i], r_lnf], writes=[r_yt[i]])
                    P.op('dve', R.tensor_tensor(out=yt[i][0:TW, :], in0=yt[i][0:TW, :], in1=lnf[0:TW, 1, :], op=ALU.add), reads=[r_yt[i], r_lnf], writes=[r_yt[i]])
                    P.dma('act', R.dma_start(out=y_out[t_base + tt * TW:t_base + (tt + 1) * TW, :], in_=yt[i][0:TW, :]), reads=[r_yt[i]], writes=[r_y])
                P.barrier()
    nc._dbg = dict(bo0=bo[0].name, bo1=bo[1].name, bo2=bo[2].name, bo3=bo[3].name, hT=hT.name, xT=xT.name, ypre="ypre")
    P.finish()
    return nc


def make_consts():
    c = np.zeros((128, CW), np.float32)
    c[:, C_ID:C_ID + 128] = np.eye(128, dtype=np.float32)
    s = np.arange(128)
    c[:, C_TRI:C_TRI + 128] = (s[:, None] <= s[None, :]).astype(np.float32)
    c[:, C_ONE:C_ONE + 128] = 1.0
    k = np.arange(16)
    c[0:16, C_BLKM:C_BLKM + 16] = ((k[:, None] // 4 == k[None, :] // 4) & (k[:, None] <= k[None, :])).astype(np.float32)
    g = np.arange(64)
    c[0:64, C_UT:C_UT + 64] = (g[:, None] > g[None, :]).astype(np.float32)
    c[:, C_IOTA] = s.astype(np.float32)
    return c


def make_mask():
    s = np.arange(128)
    q = np.arange(512)
    m = np.zeros((128, 2048), np.float32)
    for d in range(4):
        m[:, d * 512:(d + 1) * 512] = ((s[:, None] + 128 * d) <= q[None, :]).astype(np.float32)
    return m


_NC = {}


def kernel(x_prompt, x_sample, mem_prompt, cache_k, cache_v, cache_logf, cache_mem_k, cache_mem_v,
           state_conv, page_table, w_in, w_mem_k, w_mem_v, ln_v_g, ln_v_b, w_s, b_s, w_dw, b_dw,
           ln_c_g, ln_c_b, w_pw, b_pw, b_f, w_branch, w_out, ln_g, ln_b, _PH=ALL_PH):
    f = lambda a: np.ascontiguousarray(np.asarray(a, dtype=np.float32))
    x_prompt = f(x_prompt); x_sample = f(x_sample); mem_prompt = f(mem_prompt)
    depth = int(np.shape(w_in)[0]); seq = int(x_prompt.shape[1]); npool = int(np.shape(cache_k)[1]); npg = int(np.shape(page_table)[1])
    key = (depth, seq // TP, npg, npool, tuple(_PH))
    if key not in _NC:
        _NC[key] = build(DEPTH=depth, NPASS=seq // TP, NPG=npg, NPOOL=npool, PH=_PH)
    nc = _NC[key]
    shared = dict(cache_k=f(cache_k), cache_v=f(cache_v), cache_logf=f(cache_logf), w_in=f(w_in), w_mem_k=f(w_mem_k),
                  w_mem_v=f(w_mem_v), ln_v_g=f(ln_v_g), ln_v_b=f(ln_v_b), w_s=f(w_s), b_s=f(b_s), w_dw=f(w_dw), b_dw=f(b_dw),
                  ln_c_g=f(ln_c_g), ln_c_b=f(ln_c_b), w_pw=f(w_pw), b_pw=f(b_pw), b_f=f(b_f), w_branch=f(w_branch),
                  w_out=f(w_out), ln_g=f(ln_g), ln_b=f(ln_b), cst=make_consts(), cmask=make_mask())
    cmk = f(cache_mem_k).reshape(depth, 32, 256, 512); cmv = f(cache_mem_v).reshape(depth, 32, 256, 512)
    sc = f(state_conv); ptab = np.ascontiguousarray(np.asarray(page_table, dtype=np.int32))
    in_maps = []
    for c in range(8):
        b = c % 2
        m = dict(shared)
        m.update(xp=x_prompt[b], xs=np.ascontiguousarray(x_sample[4 * c:4 * c + 4].reshape(NS, D)), memp=mem_prompt[b],
                 cmk=np.ascontiguousarray(cmk[:, 4 * c:4 * c + 4]), cmv=np.ascontiguousarray(cmv[:, 4 * c:4 * c + 4]),
                 sconv=np.ascontiguousarray(sc[:, 4 * c:4 * c + 4]), pt=np.ascontiguousarray(ptab[4 * c:4 * c + 4]))
        in_maps.append(m)
    import os as _os
    if _os.environ.get("KTRACE"):
        _r = run_bass_kernel_spmd(nc, in_maps, core_ids=list(range(8)), trace=True)
        print("EXEC_NS", _r.exec_time_ns)
        res = _r.results
    else:
        res = run_bass_kernel_spmd(nc, in_maps, core_ids=list(range(8))).results
    R_ = lambda name, cores: [res[c][name] for c in cores]
    y_prompt = np.stack(R_("y_p", [0, 1]))
    y_sample = np.concatenate([r.reshape(4, 4, D) for r in R_("y_s", range(8))])
    st2 = lambda name: np.stack(R_(name, [0, 1]), axis=1)
    nk_p = st2("nk_p").reshape(depth, 2, seq, 4, 128); nv_p = st2("nv_p").reshape(depth, 2, seq, 4, 128)
    nlf_p = st2("nlf_p"); ncv_p = st2("ncv_p")
    nmk_p = st2("nmk_p").reshape(depth, 2, 256, 4, 128); nmv_p = st2("nmv_p").reshape(depth, 2, 256, 4, 128)
    cat = lambda name, shp: np.concatenate([r.reshape(shp) for r in R_(name, range(8))], axis=1)
    nk_s = cat("nk_s", (depth, 4, 4, 4, 128)); nv_s = cat("nv_s", (depth, 4, 4, 4, 128))
    nlf_s = cat("nlf_s", (depth, 4, 4, 4)); ncv_s = cat("ncv_s", (depth, 4, 30, BW)); nch_s = cat("nch_s", (depth, 4, 4, BW))
    return (y_prompt, y_sample, nk_p, nv_p, nlf_p, ncv_p, nmk_p, nmv_p, nk_s, nv_s, nlf_s, ncv_s, nch_s)
```
